# Optimizing a Trainium2 kernel written in Bass

```python
import math
import jax
import jax.numpy as jnp
from jax import lax
import numpy as np

D_MODEL = 2048
BATCH = 1
SEQ = 16384
DEPTH = 1

D_MIX = D_MODEL
D_POOL = D_MIX // 2
POOL_WINDOWS = (2, 4, 8, 16)
N_POOL_GROUPS = len(POOL_WINDOWS)
POOL_GROUP = D_POOL // N_POOL_GROUPS
D_ATTN = D_MIX - D_POOL
HEAD_DIM = 128
N_HEADS = D_ATTN // HEAD_DIM
IDX_HEADS = 16
IDX_DIM = 64
TOPK_MAX = 256
Q_BLOCK = 128
D_FF = 5632
CONV_WIDTH = 3
EPS = 1e-6

D_IN = D_POOL + 3 * D_ATTN + IDX_HEADS * IDX_DIM + IDX_DIM + IDX_HEADS

kernel_name = "hybrid_pool_dsa_convffn_block"


def rmsnorm(x, g):
    xf = x.astype(jnp.float32)
    y = xf * lax.rsqrt(jnp.mean(xf * xf, axis=-1, keepdims=True) + EPS)
    return (y * g.astype(jnp.float32)).astype(x.dtype)


def pool_mixer(u, pool_w, pool_scale):
    B, S, C = u.shape
    uf = u.astype(jnp.float32)
    csum = jnp.concatenate([jnp.zeros((B, 1, C), jnp.float32), jnp.cumsum(uf, axis=1)], axis=1)
    t = jnp.arange(S)
    outs = []
    for gi, w in enumerate(POOL_WINDOWS):
        lo = jnp.maximum(t + 1 - w, 0)
        cnt = (t + 1 - lo).astype(jnp.float32)[None, :, None]
        c_g = csum[:, :, gi * POOL_GROUP:(gi + 1) * POOL_GROUP]
        mean = (c_g[:, 1:] - c_g[:, lo]) / cnt
        outs.append(mean - uf[:, :, gi * POOL_GROUP:(gi + 1) * POOL_GROUP])
    d = jnp.stack(outs, axis=2).astype(u.dtype)
    y = jnp.einsum('bsgc,gcd->bsgd', d, pool_w).reshape(B, S, C)
    return y * pool_scale


def dsa_attention(q, k, v, q_idx, k_idx, w_idx):
    B, S, H, Dh = q.shape
    topk = min(TOPK_MAX, S // 4)
    n_blocks = S // Q_BLOCK
    key_pos = jnp.arange(S)
    scale = Dh ** -0.5
    neg = jnp.finfo(jnp.float32).min

    def one_block(blk):
        start = blk * Q_BLOCK
        qb = lax.dynamic_slice_in_dim(q, start, Q_BLOCK, axis=1)
        qib = lax.dynamic_slice_in_dim(q_idx, start, Q_BLOCK, axis=1)
        wb = lax.dynamic_slice_in_dim(w_idx, start, Q_BLOCK, axis=1)
        q_pos = start + jnp.arange(Q_BLOCK)
        rel = jax.nn.relu(jnp.einsum('bthd,bsd->bths', qib.astype(jnp.float32), k_idx.astype(jnp.float32)))
        iscore = jnp.einsum('bths,bth->bts', rel, wb.astype(jnp.float32))
        causal = key_pos[None, :] <= q_pos[:, None]
        iscore = jnp.where(causal[None], iscore, neg)
        _, sel = lax.top_k(iscore, topk)
        valid = sel <= q_pos[None, :, None]
        k_sel = jax.vmap(lambda kb, ib: kb[ib])(k, sel)
        v_sel = jax.vmap(lambda vb, ib: vb[ib])(v, sel)
        s = jnp.einsum('bthd,btjhd->bhtj', qb, k_sel).astype(jnp.float32) * scale
        s = jnp.where(valid[:, None], s, neg)
        p = jax.nn.softmax(s, axis=-1).astype(v.dtype)
        return jnp.einsum('bhtj,btjhd->bthd', p, v_sel)

    out = lax.map(one_block, jnp.arange(n_blocks, dtype=jnp.int32))
    return jnp.transpose(out, (1, 0, 2, 3, 4)).reshape(B, S, H * Dh)


def conv_ffn(h, w_up, conv_w, conv_b, w_down):
    B, S, _ = h.shape
    up = h @ w_up
    up_p = jnp.pad(up, ((0, 0), (CONV_WIDTH - 1, 0), (0, 0)))
    c = conv_b + sum(conv_w[j] * up_p[:, j:j + S] for j in range(CONV_WIDTH))
    gate, val = jnp.split(c, 2, axis=-1)
    return (jax.nn.silu(gate) * val) @ w_down


def setup_inputs(seed: int = 0) -> dict:
    key = jax.random.key(seed)
    ks = jax.random.split(key, 14)
    f32 = jnp.float32
    nrm = lambda k, shape, s: jax.random.normal(k, shape, f32) * s
    return {
        "x": nrm(ks[0], (BATCH, SEQ, D_MODEL), 1.0),
        "attn_norm_g": 1.0 + nrm(ks[1], (D_MODEL,), 0.02),
        "w_in": nrm(ks[2], (D_MODEL, D_IN), D_MODEL ** -0.5),
        "pool_w": nrm(ks[3], (N_POOL_GROUPS, POOL_GROUP, POOL_GROUP), POOL_GROUP ** -0.5),
        "pool_scale": 1.0 + nrm(ks[4], (D_POOL,), 0.02),
        "q_norm_g": 1.0 + nrm(ks[5], (HEAD_DIM,), 0.02),
        "k_norm_g": 1.0 + nrm(ks[6], (HEAD_DIM,), 0.02),
        "w_out": nrm(ks[7], (D_MIX, D_MODEL), D_MIX ** -0.5),
        "ffn_norm_g": 1.0 + nrm(ks[8], (D_MODEL,), 0.02),
        "w_up": nrm(ks[9], (D_MODEL, 2 * D_FF), D_MODEL ** -0.5),
        "conv_w": nrm(ks[10], (CONV_WIDTH, 2 * D_FF), CONV_WIDTH ** -0.5),
        "conv_b": nrm(ks[11], (2 * D_FF,), 0.01),
        "w_down": nrm(ks[12], (D_FF, D_MODEL), D_FF ** -0.5),
    }


def reference(x, attn_norm_g, w_in, pool_w, pool_scale, q_norm_g, k_norm_g, w_out,
              ffn_norm_g, w_up, conv_w, conv_b, w_down):
    B, S, _ = x.shape
    for _layer in range(DEPTH):
        h = rmsnorm(x, attn_norm_g)
        z = h @ w_in
        o = 0
        u_pool = z[..., o:o + D_POOL]; o += D_POOL
        q = z[..., o:o + D_ATTN].reshape(B, S, N_HEADS, HEAD_DIM); o += D_ATTN
        k = z[..., o:o + D_ATTN].reshape(B, S, N_HEADS, HEAD_DIM); o += D_ATTN
        v = z[..., o:o + D_ATTN].reshape(B, S, N_HEADS, HEAD_DIM); o += D_ATTN
        q_idx = z[..., o:o + IDX_HEADS * IDX_DIM].reshape(B, S, IDX_HEADS, IDX_DIM); o += IDX_HEADS * IDX_DIM
        k_idx = z[..., o:o + IDX_DIM]; o += IDX_DIM
        w_idx = z[..., o:o + IDX_HEADS] * (IDX_HEADS ** -0.5) * (IDX_DIM ** -0.5)
        q = rmsnorm(q, q_norm_g)
        k = rmsnorm(k, k_norm_g)
        y_pool = pool_mixer(u_pool, pool_w, pool_scale)
        y_attn = dsa_attention(q, k, v, q_idx, k_idx, w_idx)
        x = x + jnp.concatenate([y_pool, y_attn], axis=-1) @ w_out
        x = x + conv_ffn(rmsnorm(x, ffn_norm_g), w_up, conv_w, conv_b, w_down)
    return x
```

```python
import numpy as np
import concourse.bass as bass
import concourse.mybir as mybir
from concourse.bass_utils import run_bass_kernel_spmd

F32 = mybir.dt.float32
BF16 = mybir.dt.bfloat16
ALU = mybir.AluOpType
AF = mybir.ActivationFunctionType
AX = mybir.AxisListType


class _Rec:
    def __getattr__(self, name):
        def f(*a, **k):
            self.call = (name, a, k)
        return f


class Sched:
    ENGS = ("pe", "act", "dve", "pool", "sp")

    def __init__(self, nc):
        self.nc = nc
        self.ops = []
        self.last_w = {}
        self.readers = {}

    def _add(self, eng, fn, reads, writes, dma, sem):
        deps = set()
        for k in reads:
            if k in self.last_w:
                deps.add(self.last_w[k])
        for k in writes:
            if k in self.last_w:
                deps.add(self.last_w[k])
            deps.update(self.readers.get(k, ()))
        idx = len(self.ops)
        self.ops.append(dict(eng=eng, fn=fn, deps=deps, dma=dma, sem=sem, sig=False))
        for k in reads:
            self.readers.setdefault(k, []).append(idx)
        for k in writes:
            self.last_w[k] = idx
            self.readers[k] = []
        return idx

    def op(self, eng, fn, reads=(), writes=()):
        rec = _Rec()
        fn(rec)
        name, a, k = rec.call
        return self._add(eng, lambda e: getattr(e, name)(*a, **k), list(reads), list(writes), False, None)

    def dma(self, eng, out, in_, reads=(), writes=(), sem=None, slow=False):
        writes = list(writes)
        if sem is None:
            sem = writes[0]
        if slow:
            fn = lambda e: e.dma_start(out=out, in_=in_, allow_slow_non_contiguous=True)
        else:
            fn = lambda e: e.dma_start(out=out, in_=in_)
        return self._add(eng, fn, list(reads), writes, True, sem)

    def barrier(self):
        deps = set()
        last = {}
        for i, o in enumerate(self.ops):
            if o["dma"]:
                if not o.get("barriered"):
                    deps.add(i)
                    o["barriered"] = True
            elif o["fn"] is not None:
                last[o["eng"]] = i
        deps.update(last.values())
        for e in self.ENGS:
            self.ops.append(dict(eng=e, fn=None, deps=set(deps), dma=False, sem=None, sig=False))
        self.last_w = {}
        self.readers = {}

    def emit(self):
        nc = self.nc
        ops = self.ops
        for o in ops:
            for d in o["deps"]:
                p = ops[d]
                if p["dma"]:
                    continue
                if p["eng"] == "pe" and o["eng"] == "pe" and not o["dma"]:
                    continue
                p["sig"] = True
        esem = {e: nc.alloc_semaphore("sem_" + e) for e in self.ENGS}
        ecount = {e: 0 for e in self.ENGS}
        dsem = {}
        dcount = {}
        for o in ops:
            if o["dma"]:
                k = o["sem"]
                if k not in dsem:
                    dsem[k] = nc.alloc_semaphore("dsem_%d" % len(dsem))
                    dcount[k] = 0
                dcount[k] += 16
                o["val"] = dcount[k]
            elif o["sig"]:
                ecount[o["eng"]] += 1
                o["val"] = ecount[o["eng"]]
        per_eng = {e: [] for e in self.ENGS}
        for i, o in enumerate(ops):
            per_eng[o["eng"]].append(i)
        seen = {e: {} for e in self.ENGS}

        def run(e_name, eng):
            sn = seen[e_name]
            for i in per_eng[e_name]:
                o = ops[i]
                need = {}
                for d in o["deps"]:
                    p = ops[d]
                    if p["dma"]:
                        key = ("d", p["sem"])
                        s = dsem[p["sem"]]
                    else:
                        if p["eng"] == "pe" and e_name == "pe" and not o["dma"]:
                            continue
                        key = ("e", p["eng"])
                        s = esem[p["eng"]]
                    v = p["val"]
                    if sn.get(key, 0) >= v:
                        continue
                    if key not in need or need[key][1] < v:
                        need[key] = (s, v)
                for key, (s, v) in need.items():
                    eng.wait_ge(s, v)
                    sn[key] = v
                if o["fn"] is None:
                    continue
                ins = o["fn"](eng)
                if o["dma"]:
                    ins.then_inc(dsem[o["sem"]], 16)
                elif o["sig"]:
                    ins.then_inc(esem[e_name], 1)
            if e_name == "sp":
                for k, s in dsem.items():
                    if sn.get(("d", k), 0) < dcount[k]:
                        eng.wait_ge(s, dcount[k])

        with nc.Block() as block:
            @block.tensor
            def _(e):
                run("pe", e)

            @block.scalar
            def _(e):
                run("act", e)

            @block.vector
            def _(e):
                run("dve", e)

            @block.gpsimd
            def _(e):
                run("pool", e)

            @block.sync
            def _(e):
                run("sp", e)


D = 2048
DFF = 5632
DIN = 5200
EPS = 1e-6
NIT = 16
TOPK = 256
NEG = -1.0e30


def _ceil(a, b):
    return -(-a // b)


class Arena:
    def __init__(self, nc, kib):
        self.cap = kib * 256
        self.t = nc.alloc_sbuf_tensor("arena", [128, self.cap], F32)
        self.off = 0

    def alloc(self, shape, dtype=F32, parts=128):
        n = 1
        for d in shape:
            n *= d
        words = (n + 1) // 2 if dtype == BF16 else n
        assert self.off + words <= self.cap, ("arena overflow", self.off, words, self.cap)
        ap = self.t[0:parts, self.off:self.off + words]
        self.off += words
        if dtype == BF16:
            ap = ap.bitcast(BF16)
            if n % 2:
                ap = ap[:, 0:n]
        if len(shape) == 2:
            ap = ap.rearrange("p (a b) -> p a b", a=shape[0])
        elif len(shape) == 3:
            ap = ap.rearrange("p (a b c) -> p a b c", a=shape[0], b=shape[1])
        return ap


def slot_geometry(S_len):
    ntile = _ceil(S_len, 126)
    nslot = _ceil(ntile, 8)
    ext, mlo = [], []
    for j in range(nslot):
        e = min(S_len, 128 * _ceil(1008 * j + 1008, 128))
        ext.append(e)
        mlo.append(min(e - 128, max(0, ((1008 * j - 2) // 128) * 128)))
    mw = max(e - m for e, m in zip(ext, mlo))
    return nslot, ext, mlo, mw


def build_program(S_len, dbg=False):
    NT = S_len // 128
    NSLOT, EXT, MLO, MW = slot_geometry(S_len)
    nc = bass.Bass("TRN2", target_bir_lowering=False)

    def din(name, shape, dt=F32):
        return nc.dram_tensor(name, list(shape), dt, kind="ExternalInput").ap()

    def dscr(name, shape, dt):
        return nc.dram_tensor(name, list(shape), dt, kind="Internal").ap()

    x_all = din("x_all", [S_len, D])
    x_own = din("x_own", [NSLOT, 143, D])
    tpos_d = din("tpos", [128, NSLOT])
    bando_d = din("band_o", [128, 2 * 4 * 128])
    bandh_d = din("band_h", [15, 2 * 4 * 128])
    siota_d = din("siota", [128, MW])
    pw2_d = din("pw2", [128, NIT])
    g_attn_d = din("g_attn", [D])
    g_ffn_d = din("g_ffn", [D])
    w_in = din("w_in", [D, DIN])
    pool_w = din("pool_w", [4, 256, 256])
    pscale_d = din("pscale", [128, 8])
    gq_d = din("gq", [128, 1])
    gk_d = din("gk", [128, 1])
    w_out = din("w_out", [D, D])
    w_up = din("w_up", [D, 2 * DFF])
    cw_d = din("cw", [128, 3 * 88])
    cb_d = din("cb", [128, 88])
    w_down = din("w_down", [DFF, D])
    out_own = nc.dram_tensor("out_own", [NSLOT, 128, D], F32, kind="ExternalOutput").ap()

    kT_d = dscr("kT_d", [NT, 128, 1024], BF16)
    v_d = dscr("v_d", [NT, 128, 8 * 129], BF16)
    qT_d = dscr("qT_d", [NSLOT, 128, 1024], BF16)
    qiT_d = dscr("qiT_d", [NSLOT, 128, 1024], BF16)
    wi_d = dscr("wi_d", [NSLOT, 128, 16], F32)
    yT_d = dscr("yT_d", [NSLOT, 128, 16, 128], BF16)
    x1_d = dscr("x1_d", [NSLOT, 128, D], F32)
    hnT_d = dscr("hnT_d", [NSLOT, 128, 16, 128], BF16)

    S = Sched(nc)
    A = Arena(nc, 204)
    PS2 = [nc.alloc_psum_tensor("ps%d" % i, [128, 1024], F32) for i in range(4)]

    def bank(b):
        return PS2[b // 2][:, (b % 2) * 512:(b % 2) * 512 + 512]

    def bankb(b):
        return bank(b).bitcast(BF16)

    def pk(b):
        return ("ps", b)

    ident = A.alloc([128], BF16)
    identf = A.alloc([128], F32)
    ones_bf = A.alloc([128], BF16)
    tpos = A.alloc([NSLOT], F32)
    pw2 = A.alloc([NIT], F32)
    pscale = A.alloc([8], F32)
    gq = A.alloc([1], F32)
    gk = A.alloc([1], F32)
    cw = A.alloc([3, 88], F32)
    cb = A.alloc([88], F32)
    stat = A.alloc([64], F32)

    S.op('pool', lambda e: e.memset(identf, 0.0), writes=['identf'])
    S.op('pool', lambda e: e.affine_select(out=identf, in_=identf, pattern=[[-1, 128]], compare_op=ALU.not_equal,
                                           fill=1.0, base=0, channel_multiplier=1), reads=['identf'], writes=['identf'])
    S.op('dve', lambda e: e.tensor_copy(out=ident, in_=identf), reads=['identf'], writes=['ident'])
    S.op('dve', lambda e: e.memset(ones_bf, 1.0), writes=['ones_bf'])
    S.dma('sp', tpos, tpos_d, writes=['tpos'])
    S.dma('sp', pw2, pw2_d, writes=['pw2'])
    S.dma('sp', pscale, pscale_d, writes=['pscale'])
    S.dma('sp', gq, gq_d, writes=['gq'])
    S.dma('sp', gk, gk_d, writes=['gk'])
    S.dma('sp', cw, cw_d.rearrange("p (a b) -> p a b", a=3), writes=['cw'])
    S.dma('sp', cb, cb_d, writes=['cb'])

    rr = {"cast": 0, "evac": 0}
    G = {}

    def alloc_g(src_d):
        g_ = A.alloc([D], F32)
        G['junk'] = A.alloc([D], BF16)
        S.dma('sp', g_, src_d.partition_broadcast(128), writes=['g_attn'])
        return g_

    def cast_op(out, in_, reads, writes):
        k = rr["cast"] % 3
        rr["cast"] += 1
        if k == 0:
            S.op('act', lambda e: e.activation(out=out, in_=in_, func=AF.Copy), reads=reads, writes=writes)
        elif k == 1:
            S.op('pool', lambda e: e.tensor_copy(out=out, in_=in_), reads=reads, writes=writes)
        else:
            S.op('dve', lambda e: e.tensor_copy(out=out, in_=in_), reads=reads, writes=writes)

    def evac(out, in_, reads, writes):
        k = rr["evac"] % 2
        rr["evac"] += 1
        if k == 0:
            S.op('act', lambda e: e.activation(out=out, in_=in_, func=AF.Copy), reads=reads, writes=writes)
        else:
            S.op('dve', lambda e: e.tensor_copy(out=out, in_=in_), reads=reads, writes=writes)

    wst_n = [0]

    def load_w(dst, src, c0, ncols, key, wst, rows0=0):
        for kc in range(16):
            s = wst_n[0] % len(wst)
            wst_n[0] += 1
            st = wst[s][:, 0:ncols]
            S.dma('sp', st, src[rows0 + kc * 128:rows0 + (kc + 1) * 128, c0:c0 + ncols], writes=[('wst', s)])
            cast_op(dst[:, kc, 0:ncols], st, [('wst', s)], [(key, kc)])

    def rms_h(x_ap, xkeys, rows, g_bc, h_ap, hkey, sc):
        ssq = stat[0:rows, sc:sc + 1]
        ms = stat[0:rows, sc + 1:sc + 2]
        rstd = stat[0:rows, sc + 2:sc + 3]
        S.op('act', lambda e: e.activation(out=G['junk'][0:rows, :], in_=x_ap, func=AF.Square, accum_out=ssq),
             reads=xkeys, writes=[('st', sc), 'junk'])
        S.op('dve', lambda e: e.tensor_scalar(out=ms, in0=ssq, scalar1=1.0 / D, scalar2=EPS, op0=ALU.mult, op1=ALU.add),
             reads=[('st', sc)], writes=[('st', sc + 1)])
        S.op('act', lambda e: e.activation(out=ms, in_=ms, func=AF.Sqrt), reads=[('st', sc + 1)], writes=[('st', sc + 1)])
        S.op('dve', lambda e: e.reciprocal(out=rstd, in_=ms), reads=[('st', sc + 1)], writes=[('st', sc + 2)])
        S.op('dve', lambda e: e.scalar_tensor_tensor(out=h_ap, in0=x_ap, scalar=rstd, in1=g_bc[0:rows, :],
                                                     op0=ALU.mult, op1=ALU.mult),
             reads=list(xkeys) + [('st', sc + 2), 'g_attn'], writes=[hkey])

    def transp16(h_ap, hkey, rows, dst, dkey, col0, banks):
        for g4 in range(4):
            b = banks[g4 % len(banks)]
            pb = bankb(b)
            for q in range(4):
                kc = g4 * 4 + q
                S.op('pe', lambda e, q=q, kc=kc, pb=pb: e.transpose(pb[:, q * 128:q * 128 + rows],
                                                                 h_ap[0:rows, kc * 128:(kc + 1) * 128],
                                                                 ident[0:rows, 0:rows]),
                     reads=[hkey, 'ident'], writes=[pk(b)])
            src = pb[:, 0:512].rearrange("p (a b) -> p a b", a=4)[:, :, 0:rows]
            evac(dst[:, g4 * 4:(g4 + 1) * 4, col0:col0 + rows], src, [pk(b)], [(dkey, g4)])

    def head_norm_T(banks2, sc, gain, gkey, dstT, dkey, trbank, kn, defer=False):
        kss = stat[:, sc:sc + 8]
        kms = stat[:, sc + 8:sc + 16]
        for h in range(8):
            b = banks2[h // 4]
            src = bank(b)[:, (h % 4) * 128:(h % 4) * 128 + 128]
            S.op('act', lambda e, src=src, h=h: e.activation(out=G['junk'][:, 0:128], in_=src, func=AF.Square,
                                                          accum_out=kss[:, h:h + 1]),
                 reads=[pk(b)], writes=[('st', sc, h), 'junk'])
        S.op('dve', lambda e: e.tensor_scalar(out=kms, in0=kss, scalar1=1.0 / 128, scalar2=EPS, op0=ALU.mult, op1=ALU.add),
             reads=[('st', sc, h) for h in range(8)], writes=[('st', sc + 8)])
        S.op('act', lambda e: e.activation(out=kms, in_=kms, func=AF.Sqrt), reads=[('st', sc + 8)], writes=[('st', sc + 8)])
        S.op('dve', lambda e: e.reciprocal(out=kms, in_=kms), reads=[('st', sc + 8)], writes=[('st', sc + 8)])
        for h in range(8):
            b = banks2[h // 4]
            src = bank(b)[:, (h % 4) * 128:(h % 4) * 128 + 128]
            S.op('dve', lambda e, src=src, h=h: e.tensor_scalar(out=kn[:, h * 128:(h + 1) * 128], in0=src,
                                                             scalar1=kms[:, h:h + 1], scalar2=None, op0=ALU.mult),
                 reads=[pk(b), ('st', sc + 8)], writes=[('kn', h)])
        def part2():
            pb = bankb(trbank)
            for h in range(8):
                S.op('pe', lambda e, h=h: e.transpose(pb[:, h * 128:(h + 1) * 128], kn[:, h * 128:(h + 1) * 128], ident),
                     reads=[('kn', h), 'ident'], writes=[pk(trbank)])
            S.op('act', lambda e: e.activation(out=dstT, in_=pb, func=AF.Copy, scale=gain[:, 0:1]),
                 reads=[pk(trbank), gkey], writes=[dkey])
        if defer:
            return part2
        part2()

    base_off = A.off
    kiT = A.alloc([S_len], BF16)
    phase_off = A.off

    g_attn = alloc_g(g_attn_d)
    Wkv = A.alloc([16, 2048], BF16)
    Wki = A.alloc([16, 128], BF16)
    wst = [A.alloc([2048], F32) for _ in range(2)]
    xb = [A.alloc([D], F32) for _ in range(2)]
    hb = [A.alloc([D], BF16) for _ in range(2)]
    hT = [A.alloc([16, 128], BF16) for _ in range(2)]
    kn = A.alloc([1024], BF16)
    vsb = [A.alloc([8, 129], BF16) for _ in range(2)]
    for p_ in range(2):
        S.op('dve', lambda e: e.memset(vsb[p_][:, :, 128:129], 1.0), writes=[('vsb1', p_)])
    kTsb = [A.alloc([1024], BF16) for _ in range(2)]

    load_w(Wkv, w_in, 2048, 2048, 'Wkv', wst)
    for kc in range(16):
        s = wst_n[0] % 2
        wst_n[0] += 1
        st = wst[s][:, 0:64]
        S.dma('sp', st, w_in[kc * 128:(kc + 1) * 128, 5120:5184], writes=[('wst', s)])
        cast_op(Wki[:, kc, 0:64], st, [('wst', s)], [('Wki', kc, 0)])
        cast_op(Wki[:, kc, 64:128], st, [('wst', s)], [('Wki', kc, 1)])
    Wkv_keys = [('Wkv', kc) for kc in range(16)]

    def stage_a1(i):
        p = i % 2
        S.dma('sp', xb[p][:, 0:1024], x_all[i * 128:(i + 1) * 128, 0:1024], writes=[('xb', p, 0)])
        S.dma('sp', xb[p][:, 1024:2048], x_all[i * 128:(i + 1) * 128, 1024:2048], writes=[('xb', p, 1)])
        rms_h(xb[p], [('xb', p, 0), ('xb', p, 1)], 128, g_attn, hb[p], ('hb', p), 16 * p)

    def stage_a(i):
        p = i % 2
        transp16(hb[p], ('hb', p), 128, hT[p], ('hT', p), 0, [0, 1])

    stage_a1(0)
    stage_a(0)
    for i in range(NT):
        p = i % 2
        hTk = [(('hT', p), g4) for g4 in range(4)]
        if i + 1 < NT:
            stage_a1(i + 1)
        for n in range(2):
            b = 4 + n
            for kc in range(16):
                S.op('pe', lambda e, kc=kc, n=n, b=b: e.matmul(bank(b), lhsT=hT[p][:, kc, :],
                                                              rhs=Wkv[:, kc, n * 512:(n + 1) * 512],
                                                              start=(kc == 0), stop=(kc == 15)),
                     reads=hTk + [('Wkv', kc)], writes=[pk(b)])
        if i + 1 < NT:
            stage_a(i + 1)
        part2 = head_norm_T([4, 5], 32 + 16 * p, gk, 'gk', kTsb[p], ('kTsb', p), 6, kn, defer=True)
        for n in range(2):
            b = 2 + n
            for kc in range(16):
                S.op('pe', lambda e, kc=kc, n=n, b=b: e.matmul(bank(b), lhsT=hT[p][:, kc, :],
                                                              rhs=Wkv[:, kc, 1024 + n * 512:1024 + (n + 1) * 512],
                                                              start=(kc == 0), stop=(kc == 15)),
                     reads=hTk + [('Wkv', kc)], writes=[pk(b)])
            evac(vsb[p][:, n * 4:(n + 1) * 4, 0:128], bank(b).rearrange("p (h d) -> p h d", h=4), [pk(b)], [('vsb', p, n)])
        S.dma('pool', v_d[i].rearrange("p (h c) -> p h c", h=8), vsb[p], reads=[('vsb', p, 0), ('vsb', p, 1), ('vsb1', p)], writes=[('v_d', i)], sem=('vsb', p))
        for kc in range(16):
            S.op('pe', lambda e, kc=kc: e.matmul(bank(7)[:, 0:128], lhsT=Wki[:, kc, :], rhs=hT[p][:, kc, :],
                                                 start=(kc == 0), stop=(kc == 15)),
                 reads=hTk + [('Wki', kc, 0), ('Wki', kc, 1)], writes=[pk(7)])
        evac(kiT[:, i * 128:(i + 1) * 128], bank(7)[:, 0:128], [pk(7)], [('kiT', i)])
        part2()
        S.dma('pool', kT_d[i], kTsb[p], reads=[('kTsb', p)], writes=[('kT_d', i)], sem=('kTsb', p))

    S.barrier()
    A.off = phase_off

    g_attn = alloc_g(g_attn_d)
    Wp = A.alloc([16, 1024], BF16)
    Wwi = A.alloc([16, 16], BF16)
    wst = [A.alloc([2048], F32) for _ in range(2)]
    pw = A.alloc([4, 2, 256], BF16)
    bando = A.alloc([2, 4, 128], F32)
    bandh = A.alloc([2, 4, 128], F32, parts=15)
    xm = [A.alloc([D], F32) for _ in range(2)]
    xh = A.alloc([D], F32, parts=15)
    hbm = A.alloc([D], BF16)
    hbh = A.alloc([D], BF16, parts=15)
    hTq = A.alloc([16, 143], BF16)
    um = A.alloc([1024], F32)
    uh = A.alloc([1024], F32, parts=15)
    dTs = A.alloc([8, 128], BF16)
    ypT = [A.alloc([8, 128], BF16) for _ in range(2)]
    wis = [A.alloc([16], F32) for _ in range(2)]

    load_w(Wp, w_in, 0, 1024, 'Wp', wst)
    load_w(Wwi, w_in, 5184, 16, 'Wwi', wst)
    s = wst_n[0] % 2
    wst_n[0] += 1
    S.dma('sp', wst[s].rearrange("p (g c d) -> p g c d", g=4, c=2), pool_w.rearrange("g (c p) d -> p g c d", p=128),
          writes=[('wst', s)])
    cast_op(pw, wst[s].rearrange("p (g c d) -> p g c d", g=4, c=2), [('wst', s)], ['pw'])
    S.dma('sp', bando, bando_d.rearrange("p (v g t) -> p v g t", v=2, g=4), writes=['bando'])
    S.dma('sp', bandh, bandh_d.rearrange("p (v g t) -> p v g t", v=2, g=4), writes=['bandh'])

    for j in range(NSLOT):
        p = j % 2
        var = 0 if j == 0 else 1
        S.dma('sp', xm[p][:, 0:1024], x_own[j, 0:128, 0:1024], writes=[('xm', p, 0)])
        S.dma('sp', xm[p][:, 1024:2048], x_own[j, 0:128, 1024:2048], writes=[('xm', p, 1)])
        S.dma('sp', xh, x_own[j, 128:143, :], writes=['xh'])
        rms_h(xm[p], [('xm', p, 0), ('xm', p, 1)], 128, g_attn, hbm, 'hbm', 16 * p)
        rms_h(xh, ['xh'], 15, g_attn, hbh, 'hbh', 8 + 16 * p)
        transp16(hbm, 'hbm', 128, hTq, 'hTqm', 0, [0, 1])
        transp16(hbh, 'hbh', 15, hTq, 'hTqh', 128, [0, 1])
        hm = [('hTqm', g4) for g4 in range(4)]
        hh = [('hTqh', g4) for g4 in range(4)]
        for n in range(2):
            b = 2 + n
            for kc in range(16):
                S.op('pe', lambda e, kc=kc, n=n, b=b: e.matmul(bank(b), lhsT=hTq[:, kc, 0:128],
                                                              rhs=Wp[:, kc, n * 512:(n + 1) * 512],
                                                              start=(kc == 0), stop=(kc == 15)),
                     reads=hm + [('Wp', kc)], writes=[pk(b)])
            evac(um[:, n * 512:(n + 1) * 512], bank(b), [pk(b)], [('um', n)])
        for n in range(2):
            b = 4 + n
            for kc in range(16):
                S.op('pe', lambda e, kc=kc, n=n, b=b: e.matmul(bank(b)[0:15, :], lhsT=hTq[:, kc, 128:143],
                                                              rhs=Wp[:, kc, n * 512:(n + 1) * 512],
                                                              start=(kc == 0), stop=(kc == 15)),
                     reads=hh + [('Wp', kc)], writes=[pk(b)])
            evac(uh[:, n * 512:(n + 1) * 512], bank(b)[0:15, :], [pk(b)], [('uh', n)])
        for kc in range(16):
            S.op('pe', lambda e, kc=kc: e.matmul(bank(6)[:, 0:16], lhsT=hTq[:, kc, 0:128], rhs=Wwi[:, kc, :],
                                                 start=(kc == 0), stop=(kc == 15)),
                 reads=hm + [('Wwi', kc)], writes=[pk(6)])
        S.op('act', lambda e, p=p: e.activation(out=wis[p], in_=bank(6)[:, 0:16], func=AF.Copy, scale=1.0 / 32.0),
             reads=[pk(6)], writes=[('wis', p)])
        S.dma('pool', wi_d[j], wis[p], reads=[('wis', p)], writes=[('wi_d', j)], sem=('wis', p))
        for ci in range(8):
            g = ci // 2
            b = ci // 4
            o = bank(b)[:, (ci % 4) * 128:(ci % 4) * 128 + 128]
            S.op('pe', lambda e, o=o, ci=ci, g=g: e.matmul(o, lhsT=um[:, ci * 128:(ci + 1) * 128],
                                                          rhs=bando[:, var, g, :], start=True, stop=False),
                 reads=[('um', ci // 4), 'bando'], writes=[pk(b)])
            S.op('pe', lambda e, o=o, ci=ci, g=g: e.matmul(o, lhsT=uh[:, ci * 128:(ci + 1) * 128],
                                                          rhs=bandh[:, var, g, :], start=False, stop=True),
                 reads=[('uh', ci // 4), 'bandh'], writes=[pk(b)])
        for b in range(2):
            evac(dTs[:, b * 4:(b + 1) * 4, :], bank(b).rearrange("p (a t) -> p a t", a=4), [pk(b)], [('dTs', b)])
        for ci in range(8):
            g = ci // 2
            dc = ci % 2
            b = 2 + ci // 4
            o = bank(b)[:, (ci % 4) * 128:(ci % 4) * 128 + 128]
            for cc in range(2):
                S.op('pe', lambda e, o=o, g=g, dc=dc, cc=cc: e.matmul(o, lhsT=pw[:, g, cc, dc * 128:(dc + 1) * 128],
                                                                     rhs=dTs[:, 2 * g + cc, :],
                                                                     start=(cc == 0), stop=(cc == 1)),
                     reads=['pw', ('dTs', (2 * g + cc) // 4)], writes=[pk(b)])
        for ci in range(8):
            b = 2 + ci // 4
            o = bank(b)[:, (ci % 4) * 128:(ci % 4) * 128 + 128]
            S.op('dve', lambda e, o=o, ci=ci, p=p: e.tensor_scalar(out=ypT[p][:, ci, :], in0=o, scalar1=pscale[:, ci:ci + 1],
                                                                   scalar2=None, op0=ALU.mult),
                 reads=[pk(b), 'pscale'], writes=[('ypT', p, ci)])
        S.dma('pool', yT_d[j][:, 0:8, :], ypT[p], reads=[('ypT', p, ci) for ci in range(8)],
              writes=[('yT_d', j, 0)], sem=('ypT', p))

    S.barrier()
    A.off = phase_off

    g_attn = alloc_g(g_attn_d)
    Wq = A.alloc([16, 1024], BF16)
    Wqi = A.alloc([16, 1024], BF16)
    wst = [A.alloc([2048], F32) for _ in range(2)]
    xm = [A.alloc([D], F32) for _ in range(2)]
    hbm = A.alloc([D], BF16)
    hTq = A.alloc([16, 128], BF16)
    kn = A.alloc([1024], BF16)
    qTs = [A.alloc([1024], BF16) for _ in range(2)]
    qiTs = [A.alloc([1024], BF16) for _ in range(2)]
    load_w(Wq, w_in, 1024, 1024, 'Wq', wst)
    load_w(Wqi, w_in, 4096, 1024, 'Wqi', wst)
    for j in range(NSLOT):
        p = j % 2
        S.dma('sp', xm[p][:, 0:1024], x_own[j, 0:128, 0:1024], writes=[('xm', p, 0)])
        S.dma('sp', xm[p][:, 1024:2048], x_own[j, 0:128, 1024:2048], writes=[('xm', p, 1)])
        rms_h(xm[p], [('xm', p, 0), ('xm', p, 1)], 128, g_attn, hbm, 'hbm', 16 * p)
        transp16(hbm, 'hbm', 128, hTq, 'hTqm', 0, [0, 1])
        hm = [('hTqm', g4) for g4 in range(4)]
        for n in range(2):
            b = 2 + n
            for kc in range(16):
                S.op('pe', lambda e, kc=kc, n=n, b=b: e.matmul(bank(b), lhsT=hTq[:, kc, :],
                                                              rhs=Wq[:, kc, n * 512:(n + 1) * 512],
                                                              start=(kc == 0), stop=(kc == 15)),
                     reads=hm + [('Wq', kc)], writes=[pk(b)])
        for hp in range(8):
            b = 4 + hp // 4
            o = bank(b)[:, (hp % 4) * 128:(hp % 4) * 128 + 128]
            for kc in range(16):
                S.op('pe', lambda e, o=o, kc=kc, hp=hp: e.matmul(o, lhsT=Wqi[:, kc, hp * 128:(hp + 1) * 128],
                                                                rhs=hTq[:, kc, :], start=(kc == 0), stop=(kc == 15)),
                     reads=hm + [('Wqi', kc)], writes=[pk(b)])
        for b in range(2):
            evac(qiTs[p][:, b * 512:(b + 1) * 512], bank(4 + b), [pk(4 + b)], [('qiTs', p, b)])
        S.dma('pool', qiT_d[j], qiTs[p], reads=[('qiTs', p, 0), ('qiTs', p, 1)], writes=[('qiT_d', j)], sem=('qiTs', p))
        head_norm_T([2, 3], 32 + 16 * p, gq, 'gq', qTs[p], ('qTs', p), 6, kn)
        S.dma('pool', qT_d[j], qTs[p], reads=[('qTs', p)], writes=[('qT_d', j)], sem=('qTs', p))

    S.barrier()
    A.off = phase_off

    ACC = A.alloc([S_len], F32)
    NOTM = A.alloc([S_len], BF16)
    siota = A.alloc([MW], F32)
    pen = A.alloc([MW], F32)
    qT = [A.alloc([1024], BF16) for _ in range(2)]
    qiT = [A.alloc([1024], BF16) for _ in range(2)]
    wi = [A.alloc([16], F32) for _ in range(2)]
    wabs = [A.alloc([16], F32) for _ in range(2)]
    sgn = [A.alloc([16], F32) for _ in range(2)]
    Dg = [A.alloc([16, 128], BF16) for _ in range(2)]
    NRB = 6
    Rb = [A.alloc([512], BF16) for _ in range(NRB)]
    NKV = 3
    kTb = [A.alloc([1024], BF16) for _ in range(NKV)]
    vb = [A.alloc([8, 129], BF16) for _ in range(NKV)]
    PT = [A.alloc([1024], BF16) for _ in range(2)]
    negI = A.alloc([128], BF16)
    negIf = A.alloc([128], F32)
    yn = A.alloc([8, 128], BF16)
    rden8 = A.alloc([8], F32)
    yTa = [A.alloc([8, 128], BF16) for _ in range(2)]
    bs = A.alloc([64], F32)
    hw = A.alloc([NIT], F32)
    cwt = A.alloc([16], F32)
    CP = 2048
    cjunk = [A.alloc([CP], BF16) for _ in range(2)]
    cjunka = [A.alloc([1024], BF16) for _ in range(2)]
    cjn = [0, 0]
    S.dma('sp', siota, siota_d, writes=['siota'])
    S.op('dve', lambda e: e.memset(cwt[:, 0:8], 1.0), writes=['cwt0'])
    S.op('dve', lambda e: e.memset(cwt[:, 8:16], -0.5), writes=['cwt1'])
    S.op('pool', lambda e: e.memset(negIf, 0.0), writes=['negIf'])
    S.op('pool', lambda e: e.affine_select(out=negIf, in_=negIf, pattern=[[-1, 128]], compare_op=ALU.not_equal,
                                           fill=-30000.0, base=0, channel_multiplier=1), reads=['negIf'], writes=['negIf'])
    S.op('dve', lambda e: e.tensor_copy(out=negI, in_=negIf), reads=['negIf'], writes=['negI'])

    lo = bs[:, 0:1]
    rmax = bs[:, 1:2]
    mid = bs[:, 2:3]
    cnt = bs[:, 3:4]
    tmp = bs[:, 4:5]
    tl = bs[:, 5:6]
    cparts = bs[:, 16:32]
    cj2 = bs[:, 32:48]
    kvn = [0]
    evn = [0]
    SB_S, SB_A = (5, 6), 7
    PVB = (2, 3, 4)

    def pv_slice(h):
        b = PVB[h // 3]
        o = (h % 3) * 129
        return b, bank(b)[:, o:o + 129]

    def ib_cost(j):
        E = EXT[j]
        return 14.0 + NIT * (max(_ceil(max(512, 512 * int(round(E * 0.36 / 512))), 1024), _ceil(E - 512 * int(round(E * 0.36 / 512)), 2048)) * 1.1 + 1.0) + _ceil(E, 512) * 0.4

    def att_cost(j):
        return (EXT[j] // 128) * 2.0 + 6.0

    def idx_gen(j):
        p = j % 2
        E = EXT[j]
        S.dma('sp', qT[p], qT_d[j], reads=[('qT_d', j)], writes=[('qT', p)])
        S.dma('sp', qiT[p], qiT_d[j], reads=[('qiT_d', j)], writes=[('qiT', p)])
        S.dma('sp', wi[p], wi_d[j], reads=[('wi_d', j)], writes=[('wi', p)])
        S.op('act', lambda e: e.activation(out=wabs[p], in_=wi[p], func=AF.Abs), reads=[('wi', p)], writes=[('wabs', p)])
        S.op('act', lambda e: e.activation(out=sgn[p], in_=wi[p], func=AF.Sign), reads=[('wi', p)], writes=[('sgn', p)])
        for h in range(16):
            S.op('dve', lambda e, h=h: e.tensor_scalar(out=Dg[p][:, h, :], in0=identf, scalar1=sgn[p][:, h:h + 1],
                                                       scalar2=None, op0=ALU.mult),
                 reads=['identf', ('sgn', p)], writes=[('Dg', p, h)])
        nsb = _ceil(E, 512)
        steps = [(sb, hp) for sb in range(nsb) for hp in range(8)]
        pend = []
        for si, (sb, hp) in enumerate(steps):
            c0 = sb * 512
            w = min(512, E - c0)
            kik = [('kiT', c0 // 128 + t) for t in range(w // 128)]
            ab = 6 + sb % 2
            rbs = []
            for half in range(2):
                h = 2 * hp + half
                b = (si % 3) * 2 + half
                r = (si * 2 + half) % NRB
                lo_p = 64 * half
                S.op('pe', lambda e, b=b, lo_p=lo_p: e.matmul(
                    bank(b)[:, 0:w], lhsT=qiT[p][lo_p:lo_p + 64, hp * 128:(hp + 1) * 128],
                    rhs=kiT[lo_p:lo_p + 64, c0:c0 + w], start=True, stop=True),
                     reads=[('qiT', p)] + kik, writes=[pk(b)])
                if half == 1:
                    S.op('dve', lambda e, b=b, r=r, h=h: e.tensor_scalar(
                        out=Rb[r][:, 0:w], in0=bank(b)[:, 0:w], scalar1=wabs[p][:, h:h + 1], scalar2=0.0,
                        op0=ALU.mult, op1=ALU.max), reads=[pk(b), ('wabs', p)], writes=[('Rb', r)])
                else:
                    S.op('act', lambda e, b=b, r=r, h=h: e.activation(
                        out=Rb[r][:, 0:w], in_=bank(b)[:, 0:w], func=AF.Relu, scale=wabs[p][:, h:h + 1]),
                         reads=[pk(b), ('wabs', p)], writes=[('Rb', r)])
                rbs.append((h, r))

            def dmm(rbs=rbs, sb=sb, c0=c0, w=w):
                blo = 6 + sb % 2
                bhi = 7 - sb % 2
                for (h, r) in rbs:
                    S.op('pe', lambda e, h=h, r=r: e.matmul(bank(blo)[0:64, 0:w], lhsT=Dg[p][0:64, h, 0:64], rhs=Rb[r][0:64, 0:w],
                                                            start=(h == 0), stop=(h == 15)),
                         reads=[('Dg', p, h), ('Rb', r)], writes=[('psh', blo, 0)])
                    S.op('pe', lambda e, h=h, r=r: e.matmul(bank(bhi)[64:128, 0:w], lhsT=Dg[p][64:128, h, 64:128],
                                                            rhs=Rb[r][64:128, 0:w], start=(h == 0), stop=(h == 15)),
                         reads=[('Dg', p, h), ('Rb', r)], writes=[('psh', bhi, 1)])
                if rbs[-1][0] == 15:
                    S.op('act', lambda e: e.activation(out=ACC[0:64, c0:c0 + w], in_=bank(blo)[0:64, 0:w], func=AF.Copy),
                         reads=[('psh', blo, 0)], writes=[('acc', sb)])
                    S.op('dve', lambda e: e.tensor_copy(out=ACC[64:128, c0:c0 + w], in_=bank(bhi)[64:128, 0:w]),
                         reads=[('psh', bhi, 1)], writes=[('acch', sb)])
            pend.append(dmm)
            if len(pend) > 2:
                pend.pop(0)()
            yield 1.0
        for f_ in pend:
            f_()
        yield 1.0

    def ib_gen(j):
        p = j % 2
        E = EXT[j]
        nsb = _ceil(E, 512)
        acck = [('acc', sb) for sb in range(nsb)] + [('acch', sb) for sb in range(nsb)]
        S.op('dve', lambda e: e.tensor_reduce(out=rmax, in_=ACC[:, 0:E], op=ALU.max, axis=AX.X),
             reads=acck, writes=['rmax'])
        S.op('dve', lambda e: e.tensor_reduce(out=lo, in_=ACC[:, 0:E], op=ALU.min, axis=AX.X),
             reads=acck, writes=['lo'])
        yield 6.0
        ml = MLO[j]
        W = E - ml
        S.op('dve', lambda e: e.tensor_scalar(out=tl, in0=tpos[:, j:j + 1], scalar1=float(-ml), scalar2=None,
                                              op0=ALU.add), reads=['tpos'], writes=['tl'])
        S.op('dve', lambda e: e.tensor_scalar(out=pen[:, 0:W], in0=siota[:, 0:W], scalar1=tl, scalar2=NEG,
                                              op0=ALU.is_gt, op1=ALU.mult), reads=['siota', 'tl'], writes=['pen'])
        mk = [('acc', sb) for sb in range(ml // 512, nsb)] + [('acch', sb) for sb in range(ml // 512, nsb)]
        S.op('dve', lambda e: e.tensor_tensor(out=ACC[:, ml:E], in0=ACC[:, ml:E], in1=pen[:, 0:W], op=ALU.add),
             reads=mk + ['pen', 'rmax', 'lo'], writes=mk)
        S.op('dve', lambda e: e.tensor_tensor(out=tmp, in0=rmax, in1=lo, op=ALU.subtract), reads=['rmax', 'lo'], writes=['tmp'])
        S.op('dve', lambda e: e.tensor_scalar(out=hw, in0=pw2, scalar1=tmp, scalar2=None, op0=ALU.mult),
             reads=['tmp', 'pw2'], writes=['hw'])
        Ea = max(512, min(E - 512, 512 * int(round(E * 0.36 / 512))))
        CPA, CPD = 1024, 2048
        pcs_a = [(a0, min(Ea, a0 + CPA)) for a0 in range(0, Ea, CPA)]
        pcs_d = [(a0, min(E, a0 + CPD)) for a0 in range(Ea, E, CPD)]
        assert len(pcs_a) <= 8 and len(pcs_d) <= 8
        S.op('dve', lambda e: e.memset(cparts, 0.0), reads=[('cp', i) for i in range(16)], writes=[('cp', i) for i in range(16)])
        thr = TOPK - 0.75 - 0.5 * Ea
        yield 6.0
        for k in range(NIT):
            S.op('dve', lambda e, k=k: e.tensor_tensor(out=mid, in0=lo, in1=hw[:, k:k + 1], op=ALU.add),
                 reads=['lo', 'hw'], writes=['mid'])
            for pc in range(max(len(pcs_a), len(pcs_d))):
                if pc < len(pcs_a):
                    a0, a1 = pcs_a[pc]
                    ja = cjn[0] % 2
                    cjn[0] += 1
                    S.op('act', lambda e, a0=a0, a1=a1, pc=pc, ja=ja: e.activation(
                        out=cjunka[ja][:, 0:a1 - a0], in_=ACC[:, a0:a1], func=AF.Sign, bias=mid, scale=-1.0,
                        accum_out=cparts[:, 8 + pc:9 + pc]), reads=acck + ['mid'], writes=[('cp', 8 + pc), ('cja', ja)])
                if pc < len(pcs_d):
                    a0, a1 = pcs_d[pc]
                    jd = cjn[1] % 2
                    cjn[1] += 1
                    S.op('dve', lambda e, a0=a0, a1=a1, pc=pc, jd=jd: e.tensor_scalar(
                        out=cjunk[jd][:, 0:a1 - a0], in0=ACC[:, a0:a1], scalar1=mid, scalar2=None, op0=ALU.is_ge, op1=ALU.add,
                        accum_out=cparts[:, pc:pc + 1]), reads=acck + ['mid'], writes=[('cp', pc), ('cjd', jd)])
                yield 1.1
            S.op('dve', lambda e: e.scalar_tensor_tensor(out=cj2, in0=cparts, scalar=1.0, in1=cwt, op0=ALU.mult, op1=ALU.mult,
                                                         accum_out=cnt),
                 reads=[('cp', i) for i in range(16)] + ['cwt0', 'cwt1'], writes=['cnt', 'cj2'])
            S.op('dve', lambda e, k=k: e.scalar_tensor_tensor(out=tmp, in0=cnt, scalar=float(thr), in1=hw[:, k:k + 1],
                                                              op0=ALU.is_ge, op1=ALU.mult),
                 reads=['cnt', 'hw'], writes=['tmp'])
            S.op('dve', lambda e: e.tensor_tensor(out=lo, in0=lo, in1=tmp, op=ALU.add), reads=['lo', 'tmp'], writes=['lo'])
            yield 1.0
        for g in range(nsb):
            c0 = g * 512
            w = min(512, E - c0)
            S.op('dve' if g % 2 else 'pool', lambda e: e.tensor_scalar(out=NOTM[:, c0:c0 + w], in0=ACC[:, c0:c0 + w], scalar1=lo,
                                                                      scalar2=None, op0=ALU.is_lt),
                 reads=[('acc', g), ('acch', g), 'lo'], writes=[('notm', g)])
            yield 0.4

    def att_gen(j):
        p = j % 2
        E = EXT[j]
        nkb = E // 128
        pend = None
        for kb in range(nkb):
            s3 = kvn[0] % NKV
            kvn[0] += 1
            S.dma('sp', kTb[s3], kT_d[kb], reads=[('kT_d', kb)], writes=[('kTb', s3)])
            S.dma('sp', vb[s3], v_d[kb].rearrange("p (h c) -> p h c", h=8), reads=[('v_d', kb)], writes=[('vb', s3)])
            g = kb // 4
            pm = kb % 2
            for half in range(2):
                S.op('pe', lambda e, half=half: e.matmul(
                    bank(half), lhsT=NOTM[:, kb * 128:(kb + 1) * 128], rhs=negI.unsqueeze(1).to_broadcast([128, 4, 128]),
                    start=True, stop=False), reads=[('notm', g), 'negI'], writes=[pk(half)])
                for hh in range(4):
                    h = half * 4 + hh
                    S.op('pe', lambda e, h=h, hh=hh, half=half: e.matmul(
                        bank(half)[:, hh * 128:(hh + 1) * 128], lhsT=kTb[s3][:, h * 128:(h + 1) * 128],
                        rhs=qT[p][:, h * 128:(h + 1) * 128], start=False, stop=(hh == 3)),
                         reads=[('kTb', s3), ('qT', p)], writes=[pk(half)])
                if pend is not None:
                    pend()
                S.op('act', lambda e, half=half: e.activation(out=PT[pm][:, half * 512:(half + 1) * 512], in_=bank(half),
                                                              func=AF.Exp, scale=128.0 ** -0.5),
                     reads=[pk(half)], writes=[('PT', pm, half)])

                def pv(kb=kb, pm=pm, s3=s3, half=half):
                    for h in range(half * 4, half * 4 + 4):
                        b, o = pv_slice(h)
                        S.op('pe', lambda e, h=h, o=o: e.matmul(o, lhsT=PT[pm][:, h * 128:(h + 1) * 128], rhs=vb[s3][:, h, :],
                                                                start=(kb == 0 and h % 3 == 0), stop=(kb == nkb - 1)),
                             reads=[('vb', s3), ('PT', pm, half)], writes=[pk(b)])
                pend = pv
            yield 2.0
        pend()
        for bi, b in enumerate(PVB):
            nh = 3 if bi < 2 else 2
            v3 = bank(b)[:, 0:nh * 129].rearrange("p (h c) -> p h c", h=nh)
            S.op('dve', lambda e, v3=v3, bi=bi, nh=nh: e.tensor_scalar(out=rden8[:, bi * 3:bi * 3 + nh], in0=v3[:, :, 128],
                                                                      scalar1=1e-30, scalar2=None, op0=ALU.add),
                 reads=[pk(b)], writes=[('rden8', bi)])
            S.op('dve', lambda e, bi=bi, nh=nh: e.reciprocal(out=rden8[:, bi * 3:bi * 3 + nh], in_=rden8[:, bi * 3:bi * 3 + nh]),
                 reads=[('rden8', bi)], writes=[('rden8', bi)])
            S.op('dve', lambda e, v3=v3, bi=bi, nh=nh: e.tensor_tensor(
                out=yn[:, bi * 3:bi * 3 + nh, :], in0=v3[:, :, 0:128],
                in1=rden8[:, bi * 3:bi * 3 + nh].unsqueeze(2).to_broadcast([128, nh, 128]), op=ALU.mult),
                 reads=[pk(b), ('rden8', bi)], writes=[('yn', bi)])
        pb = bankb(0)
        for h in range(8):
            S.op('pe', lambda e, h=h: e.transpose(pb[:, h * 128:(h + 1) * 128], yn[:, h, :], ident),
                 reads=[('yn', h // 3), 'ident'], writes=[pk(0)])
        S.op('act', lambda e: e.activation(out=yTa[p], in_=pb.rearrange("p (h t) -> p h t", h=8), func=AF.Copy),
             reads=[pk(0)], writes=[('yTa', p)])
        S.dma('pool', yT_d[j][:, 8:16, :], yTa[p], reads=[('yTa', p)], writes=[('yT_d', j, 1)], sem=('yTa', p))
        yield 6.0

    def merge(tasks):
        prog = [0.0] * len(tasks)
        alive = [True] * len(tasks)
        while any(alive):
            best = None
            for i, (g_, tot) in enumerate(tasks):
                if alive[i] and (best is None or prog[i] / tot < prog[best] / tasks[best][1]):
                    best = i
            try:
                prog[best] += next(tasks[best][0])
            except StopIteration:
                alive[best] = False

    merge([(idx_gen(0), 1.0)])
    merge([(ib_gen(0), ib_cost(0))])
    for j in range(1, NSLOT + 1):
        if j < NSLOT:
            merge([(idx_gen(j), 1.0)])
            merge([(att_gen(j - 1), att_cost(j - 1)), (ib_gen(j), ib_cost(j))])
        else:
            merge([(att_gen(j - 1), att_cost(j - 1))])

    S.barrier()
    A.off = base_off

    g_ffn = alloc_g(g_ffn_d)
    Wo = A.alloc([16, 2048], BF16)
    wst = [A.alloc([2048], F32) for _ in range(2)]
    yT = [A.alloc([16, 128], BF16) for _ in range(2)]
    xm = [A.alloc([D], F32) for _ in range(2)]
    x1 = [A.alloc([D], F32) for _ in range(2)]
    hbm = [A.alloc([D], BF16) for _ in range(2)]
    hnTs = [A.alloc([16, 128], BF16) for _ in range(2)]
    load_w(Wo, w_out, 0, 2048, 'Wo', wst)
    def b_mm(j):
        p = j % 2
        S.dma('sp', yT[p], yT_d[j], reads=[('yT_d', j, 0), ('yT_d', j, 1)], writes=[('yT', p)])
        S.dma('sp', xm[p][:, 0:1024], x_own[j, 0:128, 0:1024], writes=[('xm', p, 0)])
        S.dma('sp', xm[p][:, 1024:2048], x_own[j, 0:128, 1024:2048], writes=[('xm', p, 1)])
        for n in range(4):
            b = n
            for kc in range(16):
                S.op('pe', lambda e, kc=kc, n=n, b=b: e.matmul(bank(b), lhsT=yT[p][:, kc, :], rhs=Wo[:, kc, n * 512:(n + 1) * 512],
                                                              start=(kc == 0), stop=(kc == 15)),
                     reads=[('yT', p), ('Wo', kc)], writes=[pk(b)])

    def b_add(j):
        p = j % 2
        for n in range(4):
            b = n
            S.op('dve', lambda e, n=n, b=b: e.tensor_tensor(out=x1[p][:, n * 512:(n + 1) * 512], in0=bank(b),
                                                            in1=xm[p][:, n * 512:(n + 1) * 512], op=ALU.add),
                 reads=[pk(b), ('xm', p, n // 2)], writes=[('x1', p, n)])
        x1k = [('x1', p, n) for n in range(4)]
        S.dma('pool', x1_d[j], x1[p], reads=x1k, writes=[('x1_d', j)], sem=('x1', p))

    def b_norm1(j):
        p = j % 2
        x1k = [('x1', p, n) for n in range(4)]
        rms_h(x1[p], x1k, 128, g_ffn, hbm[p], ('hbm', p), 16 * p)

    def b_norm2(j):
        p = j % 2
        transp16(hbm[p], ('hbm', p), 128, hnTs[p], ('hnTs', p), 0, [4, 5])
        S.dma('pool', hnT_d[j], hnTs[p], reads=[(('hnTs', p), g4) for g4 in range(4)], writes=[('hnT_d', j)], sem=('hnTs', p))

    b_mm(0)
    b_add(0)
    for j in range(NSLOT):
        b_norm1(j)
        if j + 1 < NSLOT:
            b_mm(j + 1)
        b_norm2(j)
        if j + 1 < NSLOT:
            b_add(j + 1)

    S.barrier()
    A.off = base_off

    BT = 6
    TBM = BT * 128
    gT = A.alloc([44, TBM], BF16)
    f_off = A.off
    blocks = [list(range(a, min(NSLOT, a + BT))) for a in range(0, NSLOT, BT)]
    for blk in blocks:
        nt = len(blk)
        TB = nt * 128
        A.off = f_off
        hnT = A.alloc([16, TBM], BF16)
        WG = [A.alloc([16, 256], BF16) for _ in range(2)]
        WV = [A.alloc([16, 256], BF16) for _ in range(2)]
        stg = [A.alloc([4, 256], F32) for _ in range(4)]
        cg = [A.alloc([512], F32) for _ in range(2)]
        cv = [A.alloc([512], F32) for _ in range(2)]
        sg = [A.alloc([512], F32) for _ in range(2)]
        for ti, j in enumerate(blk):
            S.dma('sp', hnT[:, :, ti * 128:(ti + 1) * 128], hnT_d[j], reads=[('hnT_d', j)], writes=[('hnT', ti)])
        hnk = [('hnT', ti) for ti in range(nt)]
        sn = 0
        cn = 0
        bn = 0
        def pre_up(fg_):
            nonlocal_sn = snh
            wp_ = fg_ % 2
            for (Wt, wkey, cbase) in ((WG, 'WG', 0), (WV, 'WV', DFF)):
                for kq in range(4):
                    s = nonlocal_sn[0] % 4
                    nonlocal_sn[0] += 1
                    c0 = cbase + fg_ * 256
                    S.dma('sp', stg[s], w_up[kq * 512:(kq + 1) * 512, c0:c0 + 256].rearrange("(a p) c -> p a c", p=128),
                          writes=[('stg', s)])
                    cast_op(Wt[wp_][:, kq * 4:(kq + 1) * 4, :], stg[s], [('stg', s)], [(wkey, wp_, kq)])

        snh = [0]
        pre_up(0)
        for fg in range(22):
            wp = fg % 2
            if fg + 1 < 22:
                pre_up(fg + 1)
            for fcl in range(2):
                fc = fg * 2 + fcl
                for t0 in range(0, TB, 512):
                    tw = min(512, TB - t0)
                    bG = bn % 8
                    bV = (bn + 1) % 8
                    bn += 2
                    c_ = cn % 2
                    cn += 1
                    for (Wt, wkey, b) in ((WG, 'WG', bG), (WV, 'WV', bV)):
                        for kc in range(16):
                            S.op('pe', lambda e, Wt=Wt, b=b, kc=kc, t0=t0, tw=tw, fcl=fcl: e.matmul(
                                bank(b)[:, 0:tw], lhsT=Wt[wp][:, kc, fcl * 128:(fcl + 1) * 128], rhs=hnT[:, kc, t0:t0 + tw],
                                start=(kc == 0), stop=(kc == 15)),
                                 reads=hnk + [(wkey, wp, kc // 4)], writes=[pk(b)])
                    for (dst, dkey, b, ci) in ((cg[c_], ('cg', c_), bG, fc), (cv[c_], ('cv', c_), bV, 44 + fc)):
                        S.op('act', lambda e, dst=dst, b=b, ci=ci, tw=tw: e.activation(
                            out=dst[:, 0:tw], in_=bank(b)[:, 0:tw], func=AF.Identity, scale=cw[:, 2, ci:ci + 1], bias=cb[:, ci:ci + 1]),
                             reads=[pk(b), 'cw', 'cb'], writes=[dkey])
                        for sh in (1, 2):
                            S.op('dve', lambda e, dst=dst, b=b, ci=ci, tw=tw, sh=sh: e.scalar_tensor_tensor(
                                out=dst[:, sh:tw], in0=bank(b)[:, 0:tw - sh], scalar=cw[:, 2 - sh, ci:ci + 1], in1=dst[:, sh:tw],
                                op0=ALU.mult, op1=ALU.add),
                                 reads=[pk(b), 'cw', dkey], writes=[dkey])
                    S.op('act', lambda e, c_=c_, tw=tw: e.activation(out=sg[c_][:, 0:tw], in_=cg[c_][:, 0:tw], func=AF.Silu),
                         reads=[('cg', c_)], writes=[('sg', c_)])
                    S.op('dve', lambda e, c_=c_, tw=tw, fc=fc, t0=t0: e.tensor_tensor(
                        out=gT[:, fc, t0:t0 + tw], in0=sg[c_][:, 0:tw], in1=cv[c_][:, 0:tw], op=ALU.mult),
                         reads=[('sg', c_), ('cv', c_)], writes=[('gT', fc)])
        S.barrier()
        A.off = f_off
        Wd = [A.alloc([44, 256], BF16) for _ in range(2)]
        stg = [A.alloc([4, 256], F32) for _ in range(4)]
        xr = [A.alloc([256], F32) for _ in range(2)]
        osb = [A.alloc([256], F32) for _ in range(2)]
        gk_all = [('gT', fc) for fc in range(44)]
        on = 0
        sn = 0
        def pre_dn(nb_):
            wp_ = nb_ % 2
            for f4 in range(11):
                s = snd[0] % 4
                snd[0] += 1
                S.dma('sp', stg[s], w_down[f4 * 512:(f4 + 1) * 512, nb_ * 256:(nb_ + 1) * 256].rearrange("(a p) c -> p a c", p=128),
                      writes=[('stg', s)])
                cast_op(Wd[wp_][:, f4 * 4:(f4 + 1) * 4, :], stg[s], [('stg', s)], [('Wd', wp_, f4)])

        snd = [0]
        pre_dn(0)
        for nb in range(8):
            wp = nb % 2
            if nb + 1 < 8:
                pre_dn(nb + 1)
            for ti, j in enumerate(blk):
                o_ = on % 2
                b = on % 8
                on += 1
                S.dma('sp', xr[o_], x1_d[j][:, nb * 256:(nb + 1) * 256], reads=[('x1_d', j)], writes=[('xr', o_)])
                for fc in range(44):
                    S.op('pe', lambda e, b=b, fc=fc, ti=ti, wp=wp: e.matmul(
                        bank(b)[:, 0:256], lhsT=gT[:, fc, ti * 128:(ti + 1) * 128], rhs=Wd[wp][:, fc, :],
                        start=(fc == 0), stop=(fc == 43)),
                         reads=[('gT', fc), ('Wd', wp, fc // 4)], writes=[pk(b)])
                S.op('dve', lambda e, b=b, o_=o_: e.tensor_tensor(out=osb[o_], in0=bank(b)[:, 0:256], in1=xr[o_], op=ALU.add),
                     reads=[pk(b), ('xr', o_)], writes=[('osb', o_)])
                S.dma('pool', out_own[j][:, nb * 256:(nb + 1) * 256], osb[o_], reads=[('osb', o_)],
                      writes=[('out', j, nb)], sem=('osb', o_))
        S.barrier()

    S.emit()
    return nc


def host_inputs(S_len, x, attn_norm_g, w_in, pool_w, pool_scale, q_norm_g, k_norm_g, w_out,
                ffn_norm_g, w_up, conv_w, conv_b, w_down):
    NSLOT, EXT, MLO, MW = slot_geometry(S_len)
    f32 = np.float32
    x2 = np.ascontiguousarray(np.asarray(x, dtype=f32).reshape(S_len, D))
    xpad = np.zeros((17 + 126 * 8 * NSLOT + 128, D), f32)
    xpad[17:17 + S_len] = x2
    siota = np.tile(np.arange(MW, dtype=f32)[None, :], (128, 1))
    pw2 = np.tile((0.5 ** np.arange(1, NIT + 1)).astype(f32)[None, :], (128, 1))
    common = {
        "x_all": x2,
        "siota": siota,
        "pw2": pw2,
        "g_attn": np.asarray(attn_norm_g, f32),
        "g_ffn": np.asarray(ffn_norm_g, f32),
        "w_in": np.asarray(w_in, f32),
        "pool_w": np.asarray(pool_w, f32),
        "pscale": np.ascontiguousarray(np.asarray(pool_scale, f32).reshape(8, 128).T),
        "gq": np.asarray(q_norm_g, f32).reshape(128, 1),
        "gk": np.asarray(k_norm_g, f32).reshape(128, 1),
        "w_out": np.asarray(w_out, f32),
        "w_up": np.asarray(w_up, f32),
        "cw": np.ascontiguousarray(np.asarray(conv_w, f32).reshape(3, 88, 128).transpose(2, 0, 1).reshape(128, 3 * 88)),
        "cb": np.ascontiguousarray(np.asarray(conv_b, f32).reshape(88, 128).T),
        "w_down": np.asarray(w_down, f32),
    }
    wins = (2, 4, 8, 16)
    in_maps = []
    for c in range(8):
        x_own = np.zeros((NSLOT, 143, D), f32)
        tpos = np.zeros((128, NSLOT), f32)
        for j in range(NSLOT):
            T = 8 * j + c
            r0 = 126 * T - 2
            x_own[j, 0:128] = xpad[17 + r0:17 + r0 + 128]
            x_own[j, 128:143] = xpad[17 + r0 - 15:17 + r0]
            tpos[:, j] = r0 + np.arange(128)
        bo = np.zeros((2, 4, 128, 128), f32)
        bh = np.zeros((2, 4, 15, 128), f32)
        for var in range(2):
            r0 = 126 * c - 2 if var == 0 else 10 ** 6
            for g, w in enumerate(wins):
                for t in range(128):
                    pos = r0 + t
                    cntv = float(min(pos + 1, w)) if pos >= 0 else 1.0
                    for i in range(w):
                        tp = t - i
                        if tp >= 0:
                            bo[var, g, tp, t] += 1.0 / cntv
                        else:
                            bh[var, g, 15 + tp, t] += 1.0 / cntv
                    bo[var, g, t, t] -= 1.0
        m = dict(common)
        m["x_own"] = x_own
        m["tpos"] = tpos
        m["band_o"] = np.ascontiguousarray(bo.transpose(2, 0, 1, 3).reshape(128, 2 * 4 * 128))
        m["band_h"] = np.ascontiguousarray(bh.transpose(2, 0, 1, 3).reshape(15, 2 * 4 * 128))
        in_maps.append(m)
    return in_maps


def assemble(S_len, results):
    NSLOT = slot_geometry(S_len)[0]
    out = np.zeros((S_len, D), np.float32)
    for c in range(8):
        o = results[c]["out_own"]
        for j in range(NSLOT):
            T = 8 * j + c
            a = 126 * T
            if a >= S_len:
                continue
            n = min(126, S_len - a)
            out[a:a + n] = o[j, 2:2 + n]
    return out


_CACHE = {}


def run(S_len, **inputs):
    if S_len not in _CACHE:
        _CACHE[S_len] = build_program(S_len)
    nc = _CACHE[S_len]
    in_maps = host_inputs(S_len, **inputs)
    res = run_bass_kernel_spmd(nc, in_maps, core_ids=list(range(8)))
    return assemble(S_len, res.results)


def kernel(x, attn_norm_g, w_in, pool_w, pool_scale, q_norm_g, k_norm_g, w_out,
           ffn_norm_g, w_up, conv_w, conv_b, w_down):
    S_len = x.shape[1]
    out = run(S_len, x=x, attn_norm_g=attn_norm_g, w_in=w_in, pool_w=pool_w, pool_scale=pool_scale,
              q_norm_g=q_norm_g, k_norm_g=k_norm_g, w_out=w_out, ffn_norm_g=ffn_norm_g, w_up=w_up,
              conv_w=conv_w, conv_b=conv_b, w_down=w_down)
    return out.reshape(1, S_len, D)
```

```python
import numpy as np
import concourse.bass as bass
import concourse.mybir as mybir
from concourse.bass_utils import run_bass_kernel_spmd

F32 = mybir.dt.float32
BF16 = mybir.dt.bfloat16
ALU = mybir.AluOpType
AF = mybir.ActivationFunctionType
AX = mybir.AxisListType


class _Rec:
    def __getattr__(self, name):
        def f(*a, **k):
            self.call = (name, a, k)
        return f


class Sched:
    ENGS = ("pe", "act", "dve", "pool", "sp")

    def __init__(self, nc):
        self.nc = nc
        self.ops = []
        self.last_w = {}
        self.readers = {}

    def _add(self, eng, fn, reads, writes, dma, sem):
        deps = set()
        for k in reads:
            if k in self.last_w:
                deps.add(self.last_w[k])
        for k in writes:
            if k in self.last_w:
                deps.add(self.last_w[k])
            deps.update(self.readers.get(k, ()))
        idx = len(self.ops)
        self.ops.append(dict(eng=eng, fn=fn, deps=deps, dma=dma, sem=sem, sig=False))
        for k in reads:
            self.readers.setdefault(k, []).append(idx)
        for k in writes:
            self.last_w[k] = idx
            self.readers[k] = []
        return idx

    def op(self, eng, fn, reads=(), writes=()):
        rec = _Rec()
        fn(rec)
        name, a, k = rec.call
        return self._add(eng, lambda e: getattr(e, name)(*a, **k), list(reads), list(writes), False, None)

    def dma(self, eng, out, in_, reads=(), writes=(), sem=None, slow=False):
        writes = list(writes)
        if sem is None:
            sem = writes[0]
        if slow:
            fn = lambda e: e.dma_start(out=out, in_=in_, allow_slow_non_contiguous=True)
        else:
            fn = lambda e: e.dma_start(out=out, in_=in_)
        return self._add(eng, fn, list(reads), writes, True, sem)

    def barrier(self):
        deps = set()
        last = {}
        for i, o in enumerate(self.ops):
            if o["dma"]:
                if not o.get("barriered"):
                    deps.add(i)
                    o["barriered"] = True
            elif o["fn"] is not None:
                last[o["eng"]] = i
        deps.update(last.values())
        for e in self.ENGS:
            self.ops.append(dict(eng=e, fn=None, deps=set(deps), dma=False, sem=None, sig=False))
        self.last_w = {}
        self.readers = {}

    def emit(self):
        nc = self.nc
        ops = self.ops
        for o in ops:
            for d in o["deps"]:
                p = ops[d]
                if p["dma"]:
                    continue
                if p["eng"] == "pe" and o["eng"] == "pe" and not o["dma"]:
                    continue
                p["sig"] = True
        esem = {e: nc.alloc_semaphore("sem_" + e) for e in self.ENGS}
        ecount = {e: 0 for e in self.ENGS}
        dsem = {}
        dcount = {}
        for o in ops:
            if o["dma"]:
                k = o["sem"]
                if k not in dsem:
                    dsem[k] = nc.alloc_semaphore("dsem_%d" % len(dsem))
                    dcount[k] = 0
                dcount[k] += 16
                o["val"] = dcount[k]
            elif o["sig"]:
                ecount[o["eng"]] += 1
                o["val"] = ecount[o["eng"]]
        per_eng = {e: [] for e in self.ENGS}
        for i, o in enumerate(ops):
            per_eng[o["eng"]].append(i)
        seen = {e: {} for e in self.ENGS}

        def run(e_name, eng):
            sn = seen[e_name]
            for i in per_eng[e_name]:
                o = ops[i]
                need = {}
                for d in o["deps"]:
                    p = ops[d]
                    if p["dma"]:
                        key = ("d", p["sem"])
                        s = dsem[p["sem"]]
                    else:
                        if p["eng"] == "pe" and e_name == "pe" and not o["dma"]:
                            continue
                        key = ("e", p["eng"])
                        s = esem[p["eng"]]
                    v = p["val"]
                    if sn.get(key, 0) >= v:
                        continue
                    if key not in need or need[key][1] < v:
                        need[key] = (s, v)
                for key, (s, v) in need.items():
                    eng.wait_ge(s, v)
                    sn[key] = v
                if o["fn"] is None:
                    continue
                ins = o["fn"](eng)
                if o["dma"]:
                    ins.then_inc(dsem[o["sem"]], 16)
                elif o["sig"]:
                    ins.then_inc(esem[e_name], 1)
            if e_name == "sp":
                for k, s in dsem.items():
                    if sn.get(("d", k), 0) < dcount[k]:
                        eng.wait_ge(s, dcount[k])

        with nc.Block() as block:
            @block.tensor
            def _(e):
                run("pe", e)

            @block.scalar
            def _(e):
                run("act", e)

            @block.vector
            def _(e):
                run("dve", e)

            @block.gpsimd
            def _(e):
                run("pool", e)

            @block.sync
            def _(e):
                run("sp", e)


D = 2048
DFF = 5632
DIN = 5200
EPS = 1e-6
NIT = 16
TOPK = 256
NEG = -1.0e30


def _ceil(a, b):
    return -(-a // b)


class Arena:
    def __init__(self, nc, kib):
        self.cap = kib * 256
        self.t = nc.alloc_sbuf_tensor("arena", [128, self.cap], F32)
        self.off = 0

    def alloc(self, shape, dtype=F32, parts=128):
        n = 1
        for d in shape:
            n *= d
        words = (n + 1) // 2 if dtype == BF16 else n
        assert self.off + words <= self.cap, ("arena overflow", self.off, words, self.cap)
        ap = self.t[0:parts, self.off:self.off + words]
        self.off += words
        if dtype == BF16:
            ap = ap.bitcast(BF16)
            if n % 2:
                ap = ap[:, 0:n]
        if len(shape) == 2:
            ap = ap.rearrange("p (a b) -> p a b", a=shape[0])
        elif len(shape) == 3:
            ap = ap.rearrange("p (a b c) -> p a b c", a=shape[0], b=shape[1])
        return ap


def slot_geometry(S_len):
    ntile = _ceil(S_len, 126)
    nslot = _ceil(ntile, 8)
    ext, mlo = [], []
    for j in range(nslot):
        e = min(S_len, 128 * _ceil(1008 * j + 1008, 128))
        ext.append(e)
        mlo.append(min(e - 128, max(0, ((1008 * j - 2) // 128) * 128)))
    mw = max(e - m for e, m in zip(ext, mlo))
    return nslot, ext, mlo, mw


def build_program(S_len, dbg=False):
    NT = S_len // 128
    NSLOT, EXT, MLO, MW = slot_geometry(S_len)
    nc = bass.Bass("TRN2", target_bir_lowering=False)

    def din(name, shape, dt=F32):
        return nc.dram_tensor(name, list(shape), dt, kind="ExternalInput").ap()

    def dscr(name, shape, dt):
        return nc.dram_tensor(name, list(shape), dt, kind="Internal").ap()

    x_all = din("x_all", [S_len, D])
    x_own = din("x_own", [NSLOT, 143, D])
    tpos_d = din("tpos", [128, NSLOT])
    bando_d = din("band_o", [128, 2 * 4 * 128])
    bandh_d = din("band_h", [15, 2 * 4 * 128])
    siota_d = din("siota", [128, MW])
    pw2_d = din("pw2", [128, NIT])
    g_attn_d = din("g_attn", [D])
    g_ffn_d = din("g_ffn", [D])
    w_in = din("w_in", [D, DIN])
    pool_w = din("pool_w", [4, 256, 256])
    pscale_d = din("pscale", [128, 8])
    gq_d = din("gq", [128, 1])
    gk_d = din("gk", [128, 1])
    w_out = din("w_out", [D, D])
    w_up = din("w_up", [D, 2 * DFF])
    cw_d = din("cw", [128, 3 * 88])
    cb_d = din("cb", [128, 88])
    w_down = din("w_down", [DFF, D])
    out_own = nc.dram_tensor("out_own", [NSLOT, 128, D], F32, kind="ExternalOutput").ap()

    kT_d = dscr("kT_d", [NT, 128, 1024], BF16)
    v_d = dscr("v_d", [NT, 128, 8 * 129], BF16)
    qT_d = dscr("qT_d", [NSLOT, 128, 1024], BF16)
    qiT_d = dscr("qiT_d", [NSLOT, 128, 1024], BF16)
    wi_d = dscr("wi_d", [NSLOT, 128, 16], F32)
    yT_d = dscr("yT_d", [NSLOT, 128, 16, 128], BF16)
    x1_d = dscr("x1_d", [NSLOT, 128, D], F32)
    hnT_d = dscr("hnT_d", [NSLOT, 128, 16, 128], BF16)

    S = Sched(nc)
    A = Arena(nc, 204)
    PS2 = [nc.alloc_psum_tensor("ps%d" % i, [128, 1024], F32) for i in range(4)]

    def bank(b):
        return PS2[b // 2][:, (b % 2) * 512:(b % 2) * 512 + 512]

    def bankb(b):
        return bank(b).bitcast(BF16)

    def pk(b):
        return ("ps", b)

    ident = A.alloc([128], BF16)
    identf = A.alloc([128], F32)
    ones_bf = A.alloc([128], BF16)
    tpos = A.alloc([NSLOT], F32)
    pw2 = A.alloc([NIT], F32)
    pscale = A.alloc([8], F32)
    gq = A.alloc([1], F32)
    gk = A.alloc([1], F32)
    cw = A.alloc([3, 88], F32)
    cb = A.alloc([88], F32)
    stat = A.alloc([64], F32)

    S.op('pool', lambda e: e.memset(identf, 0.0), writes=['identf'])
    S.op('pool', lambda e: e.affine_select(out=identf, in_=identf, pattern=[[-1, 128]], compare_op=ALU.not_equal,
                                           fill=1.0, base=0, channel_multiplier=1), reads=['identf'], writes=['identf'])
    S.op('dve', lambda e: e.tensor_copy(out=ident, in_=identf), reads=['identf'], writes=['ident'])
    S.op('dve', lambda e: e.memset(ones_bf, 1.0), writes=['ones_bf'])
    S.dma('sp', tpos, tpos_d, writes=['tpos'])
    S.dma('sp', pw2, pw2_d, writes=['pw2'])
    S.dma('sp', pscale, pscale_d, writes=['pscale'])
    S.dma('sp', gq, gq_d, writes=['gq'])
    S.dma('sp', gk, gk_d, writes=['gk'])
    S.dma('sp', cw, cw_d.rearrange("p (a b) -> p a b", a=3), writes=['cw'])
    S.dma('sp', cb, cb_d, writes=['cb'])

    rr = {"cast": 0, "evac": 0}
    G = {}

    def alloc_g(src_d):
        g_ = A.alloc([D], F32)
        G['junk'] = A.alloc([D], BF16)
        S.dma('sp', g_, src_d.partition_broadcast(128), writes=['g_attn'])
        return g_

    def cast_op(out, in_, reads, writes):
        k = rr["cast"] % 3
        rr["cast"] += 1
        if k == 0:
            S.op('act', lambda e: e.activation(out=out, in_=in_, func=AF.Copy), reads=reads, writes=writes)
        elif k == 1:
            S.op('pool', lambda e: e.tensor_copy(out=out, in_=in_), reads=reads, writes=writes)
        else:
            S.op('dve', lambda e: e.tensor_copy(out=out, in_=in_), reads=reads, writes=writes)

    def evac(out, in_, reads, writes):
        k = rr["evac"] % 2
        rr["evac"] += 1
        if k == 0:
            S.op('act', lambda e: e.activation(out=out, in_=in_, func=AF.Copy), reads=reads, writes=writes)
        else:
            S.op('dve', lambda e: e.tensor_copy(out=out, in_=in_), reads=reads, writes=writes)

    wst_n = [0]

    def load_w(dst, src, c0, ncols, key, wst, rows0=0):
        for kc in range(16):
            s = wst_n[0] % len(wst)
            wst_n[0] += 1
            st = wst[s][:, 0:ncols]
            S.dma('sp', st, src[rows0 + kc * 128:rows0 + (kc + 1) * 128, c0:c0 + ncols], writes=[('wst', s)])
            cast_op(dst[:, kc, 0:ncols], st, [('wst', s)], [(key, kc)])

    def rms_h(x_ap, xkeys, rows, g_bc, h_ap, hkey, sc):
        ssq = stat[0:rows, sc:sc + 1]
        ms = stat[0:rows, sc + 1:sc + 2]
        rstd = stat[0:rows, sc + 2:sc + 3]
        S.op('act', lambda e: e.activation(out=G['junk'][0:rows, :], in_=x_ap, func=AF.Square, accum_out=ssq),
             reads=xkeys, writes=[('st', sc), 'junk'])
        S.op('dve', lambda e: e.tensor_scalar(out=ms, in0=ssq, scalar1=1.0 / D, scalar2=EPS, op0=ALU.mult, op1=ALU.add),
             reads=[('st', sc)], writes=[('st', sc + 1)])
        S.op('act', lambda e: e.activation(out=ms, in_=ms, func=AF.Sqrt), reads=[('st', sc + 1)], writes=[('st', sc + 1)])
        S.op('dve', lambda e: e.reciprocal(out=rstd, in_=ms), reads=[('st', sc + 1)], writes=[('st', sc + 2)])
        S.op('dve', lambda e: e.scalar_tensor_tensor(out=h_ap, in0=x_ap, scalar=rstd, in1=g_bc[0:rows, :],
                                                     op0=ALU.mult, op1=ALU.mult),
             reads=list(xkeys) + [('st', sc + 2), 'g_attn'], writes=[hkey])

    def transp16(h_ap, hkey, rows, dst, dkey, col0, banks):
        for g4 in range(4):
            b = banks[g4 % len(banks)]
            pb = bankb(b)
            for q in range(4):
                kc = g4 * 4 + q
                S.op('pe', lambda e, q=q, kc=kc, pb=pb: e.transpose(pb[:, q * 128:q * 128 + rows],
                                                                 h_ap[0:rows, kc * 128:(kc + 1) * 128],
                                                                 ident[0:rows, 0:rows]),
                     reads=[hkey, 'ident'], writes=[pk(b)])
            src = pb[:, 0:512].rearrange("p (a b) -> p a b", a=4)[:, :, 0:rows]
            evac(dst[:, g4 * 4:(g4 + 1) * 4, col0:col0 + rows], src, [pk(b)], [(dkey, g4)])

    def head_norm_T(banks2, sc, gain, gkey, dstT, dkey, trbank, kn, defer=False):
        kss = stat[:, sc:sc + 8]
        kms = stat[:, sc + 8:sc + 16]
        for h in range(8):
            b = banks2[h // 4]
            src = bank(b)[:, (h % 4) * 128:(h % 4) * 128 + 128]
            S.op('act', lambda e, src=src, h=h: e.activation(out=G['junk'][:, 0:128], in_=src, func=AF.Square,
                                                          accum_out=kss[:, h:h + 1]),
                 reads=[pk(b)], writes=[('st', sc, h), 'junk'])
        S.op('dve', lambda e: e.tensor_scalar(out=kms, in0=kss, scalar1=1.0 / 128, scalar2=EPS, op0=ALU.mult, op1=ALU.add),
             reads=[('st', sc, h) for h in range(8)], writes=[('st', sc + 8)])
        S.op('act', lambda e: e.activation(out=kms, in_=kms, func=AF.Sqrt), reads=[('st', sc + 8)], writes=[('st', sc + 8)])
        S.op('dve', lambda e: e.reciprocal(out=kms, in_=kms), reads=[('st', sc + 8)], writes=[('st', sc + 8)])
        for h in range(8):
            b = banks2[h // 4]
            src = bank(b)[:, (h % 4) * 128:(h % 4) * 128 + 128]
            S.op('dve', lambda e, src=src, h=h: e.tensor_scalar(out=kn[:, h * 128:(h + 1) * 128], in0=src,
                                                             scalar1=kms[:, h:h + 1], scalar2=None, op0=ALU.mult),
                 reads=[pk(b), ('st', sc + 8)], writes=[('kn', h)])
        def part2():
            pb = bankb(trbank)
            for h in range(8):
                S.op('pe', lambda e, h=h: e.transpose(pb[:, h * 128:(h + 1) * 128], kn[:, h * 128:(h + 1) * 128], ident),
                     reads=[('kn', h), 'ident'], writes=[pk(trbank)])
            S.op('act', lambda e: e.activation(out=dstT, in_=pb, func=AF.Copy, scale=gain[:, 0:1]),
                 reads=[pk(trbank), gkey], writes=[dkey])
        if defer:
            return part2
        part2()

    base_off = A.off
    kiT = A.alloc([S_len], BF16)
    phase_off = A.off

    g_attn = alloc_g(g_attn_d)
    Wkv = A.alloc([16, 2048], BF16)
    Wki = A.alloc([16, 128], BF16)
    wst = [A.alloc([2048], F32) for _ in range(2)]
    xb = [A.alloc([D], F32) for _ in range(2)]
    hb = [A.alloc([D], BF16) for _ in range(2)]
    hT = [A.alloc([16, 128], BF16) for _ in range(2)]
    kn = A.alloc([1024], BF16)
    vsb = [A.alloc([8, 129], BF16) for _ in range(2)]
    for p_ in range(2):
        S.op('dve', lambda e: e.memset(vsb[p_][:, :, 128:129], 1.0), writes=[('vsb1', p_)])
    kTsb = [A.alloc([1024], BF16) for _ in range(2)]

    load_w(Wkv, w_in, 2048, 2048, 'Wkv', wst)
    for kc in range(16):
        s = wst_n[0] % 2
        wst_n[0] += 1
        st = wst[s][:, 0:64]
        S.dma('sp', st, w_in[kc * 128:(kc + 1) * 128, 5120:5184], writes=[('wst', s)])
        cast_op(Wki[:, kc, 0:64], st, [('wst', s)], [('Wki', kc, 0)])
        cast_op(Wki[:, kc, 64:128], st, [('wst', s)], [('Wki', kc, 1)])
    Wkv_keys = [('Wkv', kc) for kc in range(16)]

    def stage_a1(i):
        p = i % 2
        S.dma('sp', xb[p][:, 0:1024], x_all[i * 128:(i + 1) * 128, 0:1024], writes=[('xb', p, 0)])
        S.dma('sp', xb[p][:, 1024:2048], x_all[i * 128:(i + 1) * 128, 1024:2048], writes=[('xb', p, 1)])
        rms_h(xb[p], [('xb', p, 0), ('xb', p, 1)], 128, g_attn, hb[p], ('hb', p), 16 * p)

    def stage_a(i):
        p = i % 2
        transp16(hb[p], ('hb', p), 128, hT[p], ('hT', p), 0, [0, 1])

    stage_a1(0)
    stage_a(0)
    for i in range(NT):
        p = i % 2
        hTk = [(('hT', p), g4) for g4 in range(4)]
        if i + 1 < NT:
            stage_a1(i + 1)
        for n in range(2):
            b = 4 + n
            for kc in range(16):
                S.op('pe', lambda e, kc=kc, n=n, b=b: e.matmul(bank(b), lhsT=hT[p][:, kc, :],
                                                              rhs=Wkv[:, kc, n * 512:(n + 1) * 512],
                                                              start=(kc == 0), stop=(kc == 15)),
                     reads=hTk + [('Wkv', kc)], writes=[pk(b)])
        if i + 1 < NT:
            stage_a(i + 1)
        part2 = head_norm_T([4, 5], 32 + 16 * p, gk, 'gk', kTsb[p], ('kTsb', p), 6, kn, defer=True)
        for n in range(2):
            b = 2 + n
            for kc in range(16):
                S.op('pe', lambda e, kc=kc, n=n, b=b: e.matmul(bank(b), lhsT=hT[p][:, kc, :],
                                                              rhs=Wkv[:, kc, 1024 + n * 512:1024 + (n + 1) * 512],
                                                              start=(kc == 0), stop=(kc == 15)),
                     reads=hTk + [('Wkv', kc)], writes=[pk(b)])
            evac(vsb[p][:, n * 4:(n + 1) * 4, 0:128], bank(b).rearrange("p (h d) -> p h d", h=4), [pk(b)], [('vsb', p, n)])
        S.dma('pool', v_d[i].rearrange("p (h c) -> p h c", h=8), vsb[p], reads=[('vsb', p, 0), ('vsb', p, 1), ('vsb1', p)], writes=[('v_d', i)], sem=('vsb', p))
        for kc in range(16):
            S.op('pe', lambda e, kc=kc: e.matmul(bank(7)[:, 0:128], lhsT=Wki[:, kc, :], rhs=hT[p][:, kc, :],
                                                 start=(kc == 0), stop=(kc == 15)),
                 reads=hTk + [('Wki', kc, 0), ('Wki', kc, 1)], writes=[pk(7)])
        evac(kiT[:, i * 128:(i + 1) * 128], bank(7)[:, 0:128], [pk(7)], [('kiT', i)])
        part2()
        S.dma('pool', kT_d[i], kTsb[p], reads=[('kTsb', p)], writes=[('kT_d', i)], sem=('kTsb', p))

    S.barrier()
    A.off = phase_off

    g_attn = alloc_g(g_attn_d)
    Wp = A.alloc([16, 1024], BF16)
    Wwi = A.alloc([16, 16], BF16)
    wst = [A.alloc([2048], F32) for _ in range(2)]
    pw = A.alloc([4, 2, 256], BF16)
    bando = A.alloc([2, 4, 128], F32)
    bandh = A.alloc([2, 4, 128], F32, parts=15)
    xm = [A.alloc([D], F32) for _ in range(2)]
    xh = A.alloc([D], F32, parts=15)
    hbm = A.alloc([D], BF16)
    hbh = A.alloc([D], BF16, parts=15)
    hTq = A.alloc([16, 143], BF16)
    um = A.alloc([1024], F32)
    uh = A.alloc([1024], F32, parts=15)
    dTs = A.alloc([8, 128], BF16)
    ypT = [A.alloc([8, 128], BF16) for _ in range(2)]
    wis = [A.alloc([16], F32) for _ in range(2)]

    load_w(Wp, w_in, 0, 1024, 'Wp', wst)
    load_w(Wwi, w_in, 5184, 16, 'Wwi', wst)
    s = wst_n[0] % 2
    wst_n[0] += 1
    S.dma('sp', wst[s].rearrange("p (g c d) -> p g c d", g=4, c=2), pool_w.rearrange("g (c p) d -> p g c d", p=128),
          writes=[('wst', s)])
    cast_op(pw, wst[s].rearrange("p (g c d) -> p g c d", g=4, c=2), [('wst', s)], ['pw'])
    S.dma('sp', bando, bando_d.rearrange("p (v g t) -> p v g t", v=2, g=4), writes=['bando'])
    S.dma('sp', bandh, bandh_d.rearrange("p (v g t) -> p v g t", v=2, g=4), writes=['bandh'])

    for j in range(NSLOT):
        p = j % 2
        var = 0 if j == 0 else 1
        S.dma('sp', xm[p][:, 0:1024], x_own[j, 0:128, 0:1024], writes=[('xm', p, 0)])
        S.dma('sp', xm[p][:, 1024:2048], x_own[j, 0:128, 1024:2048], writes=[('xm', p, 1)])
        S.dma('sp', xh, x_own[j, 128:143, :], writes=['xh'])
        rms_h(xm[p], [('xm', p, 0), ('xm', p, 1)], 128, g_attn, hbm, 'hbm', 16 * p)
        rms_h(xh, ['xh'], 15, g_attn, hbh, 'hbh', 8 + 16 * p)
        transp16(hbm, 'hbm', 128, hTq, 'hTqm', 0, [0, 1])
        transp16(hbh, 'hbh', 15, hTq, 'hTqh', 128, [0, 1])
        hm = [('hTqm', g4) for g4 in range(4)]
        hh = [('hTqh', g4) for g4 in range(4)]
        for n in range(2):
            b = 2 + n
            for kc in range(16):
                S.op('pe', lambda e, kc=kc, n=n, b=b: e.matmul(bank(b), lhsT=hTq[:, kc, 0:128],
                                                              rhs=Wp[:, kc, n * 512:(n + 1) * 512],
                                                              start=(kc == 0), stop=(kc == 15)),
                     reads=hm + [('Wp', kc)], writes=[pk(b)])
            evac(um[:, n * 512:(n + 1) * 512], bank(b), [pk(b)], [('um', n)])
        for n in range(2):
            b = 4 + n
            for kc in range(16):
                S.op('pe', lambda e, kc=kc, n=n, b=b: e.matmul(bank(b)[0:15, :], lhsT=hTq[:, kc, 128:143],
                                                              rhs=Wp[:, kc, n * 512:(n + 1) * 512],
                                                              start=(kc == 0), stop=(kc == 15)),
                     reads=hh + [('Wp', kc)], writes=[pk(b)])
            evac(uh[:, n * 512:(n + 1) * 512], bank(b)[0:15, :], [pk(b)], [('uh', n)])
        for kc in range(16):
            S.op('pe', lambda e, kc=kc: e.matmul(bank(6)[:, 0:16], lhsT=hTq[:, kc, 0:128], rhs=Wwi[:, kc, :],
                                                 start=(kc == 0), stop=(kc == 15)),
                 reads=hm + [('Wwi', kc)], writes=[pk(6)])
        S.op('act', lambda e, p=p: e.activation(out=wis[p], in_=bank(6)[:, 0:16], func=AF.Copy, scale=1.0 / 32.0),
             reads=[pk(6)], writes=[('wis', p)])
        S.dma('pool', wi_d[j], wis[p], reads=[('wis', p)], writes=[('wi_d', j)], sem=('wis', p))
        for ci in range(8):
            g = ci // 2
            b = ci // 4
            o = bank(b)[:, (ci % 4) * 128:(ci % 4) * 128 + 128]
            S.op('pe', lambda e, o=o, ci=ci, g=g: e.matmul(o, lhsT=um[:, ci * 128:(ci + 1) * 128],
                                                          rhs=bando[:, var, g, :], start=True, stop=False),
                 reads=[('um', ci // 4), 'bando'], writes=[pk(b)])
            S.op('pe', lambda e, o=o, ci=ci, g=g: e.matmul(o, lhsT=uh[:, ci * 128:(ci + 1) * 128],
                                                          rhs=bandh[:, var, g, :], start=False, stop=True),
                 reads=[('uh', ci // 4), 'bandh'], writes=[pk(b)])
        for b in range(2):
            evac(dTs[:, b * 4:(b + 1) * 4, :], bank(b).rearrange("p (a t) -> p a t", a=4), [pk(b)], [('dTs', b)])
        for ci in range(8):
            g = ci // 2
            dc = ci % 2
            b = 2 + ci // 4
            o = bank(b)[:, (ci % 4) * 128:(ci % 4) * 128 + 128]
            for cc in range(2):
                S.op('pe', lambda e, o=o, g=g, dc=dc, cc=cc: e.matmul(o, lhsT=pw[:, g, cc, dc * 128:(dc + 1) * 128],
                                                                     rhs=dTs[:, 2 * g + cc, :],
                                                                     start=(cc == 0), stop=(cc == 1)),
                     reads=['pw', ('dTs', (2 * g + cc) // 4)], writes=[pk(b)])
        for ci in range(8):
            b = 2 + ci // 4
            o = bank(b)[:, (ci % 4) * 128:(ci % 4) * 128 + 128]
            S.op('dve', lambda e, o=o, ci=ci, p=p: e.tensor_scalar(out=ypT[p][:, ci, :], in0=o, scalar1=pscale[:, ci:ci + 1],
                                                                   scalar2=None, op0=ALU.mult),
                 reads=[pk(b), 'pscale'], writes=[('ypT', p, ci)])
        S.dma('pool', yT_d[j][:, 0:8, :], ypT[p], reads=[('ypT', p, ci) for ci in range(8)],
              writes=[('yT_d', j, 0)], sem=('ypT', p))

    S.barrier()
    A.off = phase_off

    g_attn = alloc_g(g_attn_d)
    Wq = A.alloc([16, 1024], BF16)
    Wqi = A.alloc([16, 1024], BF16)
    wst = [A.alloc([2048], F32) for _ in range(2)]
    xm = [A.alloc([D], F32) for _ in range(2)]
    hbm = A.alloc([D], BF16)
    hTq = A.alloc([16, 128], BF16)
    kn = A.alloc([1024], BF16)
    qTs = [A.alloc([1024], BF16) for _ in range(2)]
    qiTs = [A.alloc([1024], BF16) for _ in range(2)]
    load_w(Wq, w_in, 1024, 1024, 'Wq', wst)
    load_w(Wqi, w_in, 4096, 1024, 'Wqi', wst)
    for j in range(NSLOT):
        p = j % 2
        S.dma('sp', xm[p][:, 0:1024], x_own[j, 0:128, 0:1024], writes=[('xm', p, 0)])
        S.dma('sp', xm[p][:, 1024:2048], x_own[j, 0:128, 1024:2048], writes=[('xm', p, 1)])
        rms_h(xm[p], [('xm', p, 0), ('xm', p, 1)], 128, g_attn, hbm, 'hbm', 16 * p)
        transp16(hbm, 'hbm', 128, hTq, 'hTqm', 0, [0, 1])
        hm = [('hTqm', g4) for g4 in range(4)]
        for n in range(2):
            b = 2 + n
            for kc in range(16):
                S.op('pe', lambda e, kc=kc, n=n, b=b: e.matmul(bank(b), lhsT=hTq[:, kc, :],
                                                              rhs=Wq[:, kc, n * 512:(n + 1) * 512],
                                                              start=(kc == 0), stop=(kc == 15)),
                     reads=hm + [('Wq', kc)], writes=[pk(b)])
        q_part2 = head_norm_T([2, 3], 32 + 16 * p, gq, 'gq', qTs[p], ('qTs', p), 6, kn, defer=True)
        for hp in range(8):
            b = 4 + hp // 4
            o = bank(b)[:, (hp % 4) * 128:(hp % 4) * 128 + 128]
            for kc in range(16):
                S.op('pe', lambda e, o=o, kc=kc, hp=hp: e.matmul(o, lhsT=Wqi[:, kc, hp * 128:(hp + 1) * 128],
                                                                rhs=hTq[:, kc, :], start=(kc == 0), stop=(kc == 15)),
                     reads=hm + [('Wqi', kc)], writes=[pk(b)])
        for b in range(2):
            evac(qiTs[p][:, b * 512:(b + 1) * 512], bank(4 + b), [pk(4 + b)], [('qiTs', p, b)])
        S.dma('pool', qiT_d[j], qiTs[p], reads=[('qiTs', p, 0), ('qiTs', p, 1)], writes=[('qiT_d', j)], sem=('qiTs', p))
        q_part2()
        S.dma('pool', qT_d[j], qTs[p], reads=[('qTs', p)], writes=[('qT_d', j)], sem=('qTs', p))

    S.barrier()
    A.off = phase_off

    ACC = A.alloc([S_len], F32)
    NOTM = A.alloc([S_len], BF16)
    siota = A.alloc([MW], F32)
    pen = A.alloc([MW], F32)
    qT = [A.alloc([1024], BF16) for _ in range(2)]
    qiT = [A.alloc([1024], BF16) for _ in range(2)]
    wi = [A.alloc([16], F32) for _ in range(2)]
    wabs = [A.alloc([16], F32) for _ in range(2)]
    sgn = [A.alloc([16], F32) for _ in range(2)]
    Dg = [A.alloc([16, 128], BF16) for _ in range(2)]
    NRB = 6
    Rb = [A.alloc([512], BF16) for _ in range(NRB)]
    NKV = 3
    kTb = [A.alloc([1024], BF16) for _ in range(NKV)]
    vb = [A.alloc([8, 129], BF16) for _ in range(NKV)]
    PT = [A.alloc([1024], BF16) for _ in range(2)]
    negI = A.alloc([128], BF16)
    negIf = A.alloc([128], F32)
    yn = A.alloc([8, 128], BF16)
    rden8 = A.alloc([8], F32)
    yTa = [A.alloc([8, 128], BF16) for _ in range(2)]
    bs = A.alloc([64], F32)
    hw = A.alloc([NIT], F32)
    cwt = A.alloc([16], F32)
    CP = 2048
    cjunk = [A.alloc([CP], BF16) for _ in range(2)]
    cjunka = [A.alloc([1024], BF16) for _ in range(2)]
    cjn = [0, 0]
    S.dma('sp', siota, siota_d, writes=['siota'])
    S.op('dve', lambda e: e.memset(cwt[:, 0:8], 1.0), writes=['cwt0'])
    S.op('dve', lambda e: e.memset(cwt[:, 8:16], -0.5), writes=['cwt1'])
    S.op('pool', lambda e: e.memset(negIf, 0.0), writes=['negIf'])
    S.op('pool', lambda e: e.affine_select(out=negIf, in_=negIf, pattern=[[-1, 128]], compare_op=ALU.not_equal,
                                           fill=-30000.0, base=0, channel_multiplier=1), reads=['negIf'], writes=['negIf'])
    S.op('dve', lambda e: e.tensor_copy(out=negI, in_=negIf), reads=['negIf'], writes=['negI'])

    lo = bs[:, 0:1]
    rmax = bs[:, 1:2]
    mid = bs[:, 2:3]
    cnt = bs[:, 3:4]
    tmp = bs[:, 4:5]
    tl = bs[:, 5:6]
    cparts = bs[:, 16:32]
    cj2 = bs[:, 32:48]
    kvn = [0]
    evn = [0]
    SB_S, SB_A = (5, 6), 7
    PVB = (2, 3, 4)

    def pv_slice(h):
        b = PVB[h // 3]
        o = (h % 3) * 129
        return b, bank(b)[:, o:o + 129]

    def ib_cost(j):
        E = EXT[j]
        return 14.0 + NIT * (max(_ceil(max(512, 512 * int(round(E * 0.36 / 512))), 1024), _ceil(E - 512 * int(round(E * 0.36 / 512)), 2048)) * 1.1 + 1.0) + _ceil(E, 512) * 0.4

    def att_cost(j):
        return (EXT[j] // 128) * 2.0 + 6.0

    def idx_gen(j):
        p = j % 2
        E = EXT[j]
        S.dma('sp', qT[p], qT_d[j], reads=[('qT_d', j)], writes=[('qT', p)])
        S.dma('sp', qiT[p], qiT_d[j], reads=[('qiT_d', j)], writes=[('qiT', p)])
        S.dma('sp', wi[p], wi_d[j], reads=[('wi_d', j)], writes=[('wi', p)])
        S.op('act', lambda e: e.activation(out=wabs[p], in_=wi[p], func=AF.Abs), reads=[('wi', p)], writes=[('wabs', p)])
        S.op('act', lambda e: e.activation(out=sgn[p], in_=wi[p], func=AF.Sign), reads=[('wi', p)], writes=[('sgn', p)])
        for h in range(16):
            S.op('dve', lambda e, h=h: e.tensor_scalar(out=Dg[p][:, h, :], in0=identf, scalar1=sgn[p][:, h:h + 1],
                                                       scalar2=None, op0=ALU.mult),
                 reads=['identf', ('sgn', p)], writes=[('Dg', p, h)])
        nsb = _ceil(E, 512)
        steps = [(sb, hp) for sb in range(nsb) for hp in range(8)]
        pend = []
        for si, (sb, hp) in enumerate(steps):
            c0 = sb * 512
            w = min(512, E - c0)
            kik = [('kiT', c0 // 128 + t) for t in range(w // 128)]
            ab = 6 + sb % 2
            rbs = []
            for half in range(2):
                h = 2 * hp + half
                b = (si % 3) * 2 + half
                r = (si * 2 + half) % NRB
                lo_p = 64 * half
                S.op('pe', lambda e, b=b, lo_p=lo_p: e.matmul(
                    bank(b)[:, 0:w], lhsT=qiT[p][lo_p:lo_p + 64, hp * 128:(hp + 1) * 128],
                    rhs=kiT[lo_p:lo_p + 64, c0:c0 + w], start=True, stop=True),
                     reads=[('qiT', p)] + kik, writes=[pk(b)])
                if half == 1:
                    S.op('dve', lambda e, b=b, r=r, h=h: e.tensor_scalar(
                        out=Rb[r][:, 0:w], in0=bank(b)[:, 0:w], scalar1=wabs[p][:, h:h + 1], scalar2=0.0,
                        op0=ALU.mult, op1=ALU.max), reads=[pk(b), ('wabs', p)], writes=[('Rb', r)])
                else:
                    S.op('act', lambda e, b=b, r=r, h=h: e.activation(
                        out=Rb[r][:, 0:w], in_=bank(b)[:, 0:w], func=AF.Relu, scale=wabs[p][:, h:h + 1]),
                         reads=[pk(b), ('wabs', p)], writes=[('Rb', r)])
                rbs.append((h, r))

            def dmm(rbs=rbs, sb=sb, c0=c0, w=w):
                blo = 6 + sb % 2
                bhi = 7 - sb % 2
                for (h, r) in rbs:
                    S.op('pe', lambda e, h=h, r=r: e.matmul(bank(blo)[0:64, 0:w], lhsT=Dg[p][0:64, h, 0:64], rhs=Rb[r][0:64, 0:w],
                                                            start=(h == 0), stop=(h == 15)),
                         reads=[('Dg', p, h), ('Rb', r)], writes=[('psh', blo, 0)])
                    S.op('pe', lambda e, h=h, r=r: e.matmul(bank(bhi)[64:128, 0:w], lhsT=Dg[p][64:128, h, 64:128],
                                                            rhs=Rb[r][64:128, 0:w], start=(h == 0), stop=(h == 15)),
                         reads=[('Dg', p, h), ('Rb', r)], writes=[('psh', bhi, 1)])
                if rbs[-1][0] == 15:
                    S.op('act', lambda e: e.activation(out=ACC[0:64, c0:c0 + w], in_=bank(blo)[0:64, 0:w], func=AF.Copy),
                         reads=[('psh', blo, 0)], writes=[('acc', sb)])
                    S.op('dve', lambda e: e.tensor_copy(out=ACC[64:128, c0:c0 + w], in_=bank(bhi)[64:128, 0:w]),
                         reads=[('psh', bhi, 1)], writes=[('acch', sb)])
            pend.append(dmm)
            if len(pend) > 2:
                pend.pop(0)()
            yield 1.0
        for f_ in pend:
            f_()
        yield 1.0

    def ib_gen(j):
        p = j % 2
        E = EXT[j]
        nsb = _ceil(E, 512)
        acck = [('acc', sb) for sb in range(nsb)] + [('acch', sb) for sb in range(nsb)]
        S.op('dve', lambda e: e.tensor_reduce(out=rmax, in_=ACC[:, 0:E], op=ALU.max, axis=AX.X),
             reads=acck, writes=['rmax'])
        S.op('dve', lambda e: e.tensor_reduce(out=lo, in_=ACC[:, 0:E], op=ALU.min, axis=AX.X),
             reads=acck, writes=['lo'])
        yield 6.0
        ml = MLO[j]
        W = E - ml
        S.op('dve', lambda e: e.tensor_scalar(out=tl, in0=tpos[:, j:j + 1], scalar1=float(-ml), scalar2=None,
                                              op0=ALU.add), reads=['tpos'], writes=['tl'])
        S.op('dve', lambda e: e.tensor_scalar(out=pen[:, 0:W], in0=siota[:, 0:W], scalar1=tl, scalar2=NEG,
                                              op0=ALU.is_gt, op1=ALU.mult), reads=['siota', 'tl'], writes=['pen'])
        mk = [('acc', sb) for sb in range(ml // 512, nsb)] + [('acch', sb) for sb in range(ml // 512, nsb)]
        S.op('dve', lambda e: e.tensor_tensor(out=ACC[:, ml:E], in0=ACC[:, ml:E], in1=pen[:, 0:W], op=ALU.add),
             reads=mk + ['pen', 'rmax', 'lo'], writes=mk)
        S.op('dve', lambda e: e.tensor_tensor(out=tmp, in0=rmax, in1=lo, op=ALU.subtract), reads=['rmax', 'lo'], writes=['tmp'])
        S.op('dve', lambda e: e.tensor_scalar(out=hw, in0=pw2, scalar1=tmp, scalar2=None, op0=ALU.mult),
             reads=['tmp', 'pw2'], writes=['hw'])
        Ea = max(512, min(E - 512, 512 * int(round(E * 0.36 / 512))))
        CPA, CPD = 1024, 2048
        pcs_a = [(a0, min(Ea, a0 + CPA)) for a0 in range(0, Ea, CPA)]
        pcs_d = [(a0, min(E, a0 + CPD)) for a0 in range(Ea, E, CPD)]
        assert len(pcs_a) <= 8 and len(pcs_d) <= 8
        S.op('dve', lambda e: e.memset(cparts, 0.0), reads=[('cp', i) for i in range(16)], writes=[('cp', i) for i in range(16)])
        thr = TOPK - 0.75 - 0.5 * Ea
        yield 6.0
        for k in range(NIT):
            S.op('dve', lambda e, k=k: e.tensor_tensor(out=mid, in0=lo, in1=hw[:, k:k + 1], op=ALU.add),
                 reads=['lo', 'hw'], writes=['mid'])
            for pc in range(max(len(pcs_a), len(pcs_d))):
                if pc < len(pcs_a):
                    a0, a1 = pcs_a[pc]
                    ja = cjn[0] % 2
                    cjn[0] += 1
                    S.op('act', lambda e, a0=a0, a1=a1, pc=pc, ja=ja: e.activation(
                        out=cjunka[ja][:, 0:a1 - a0], in_=ACC[:, a0:a1], func=AF.Sign, bias=mid, scale=-1.0,
                        accum_out=cparts[:, 8 + pc:9 + pc]), reads=acck + ['mid'], writes=[('cp', 8 + pc), ('cja', ja)])
                if pc < len(pcs_d):
                    a0, a1 = pcs_d[pc]
                    jd = cjn[1] % 2
                    cjn[1] += 1
                    S.op('dve', lambda e, a0=a0, a1=a1, pc=pc, jd=jd: e.tensor_scalar(
                        out=cjunk[jd][:, 0:a1 - a0], in0=ACC[:, a0:a1], scalar1=mid, scalar2=None, op0=ALU.is_ge, op1=ALU.add,
                        accum_out=cparts[:, pc:pc + 1]), reads=acck + ['mid'], writes=[('cp', pc), ('cjd', jd)])
                yield 1.1
            S.op('dve', lambda e: e.scalar_tensor_tensor(out=cj2, in0=cparts, scalar=1.0, in1=cwt, op0=ALU.mult, op1=ALU.mult,
                                                         accum_out=cnt),
                 reads=[('cp', i) for i in range(16)] + ['cwt0', 'cwt1'], writes=['cnt', 'cj2'])
            S.op('dve', lambda e, k=k: e.scalar_tensor_tensor(out=tmp, in0=cnt, scalar=float(thr), in1=hw[:, k:k + 1],
                                                              op0=ALU.is_ge, op1=ALU.mult),
                 reads=['cnt', 'hw'], writes=['tmp'])
            S.op('dve', lambda e: e.tensor_tensor(out=lo, in0=lo, in1=tmp, op=ALU.add), reads=['lo', 'tmp'], writes=['lo'])
            yield 1.0
        for g in range(nsb):
            c0 = g * 512
            w = min(512, E - c0)
            S.op('dve' if g % 2 else 'pool', lambda e: e.tensor_scalar(out=NOTM[:, c0:c0 + w], in0=ACC[:, c0:c0 + w], scalar1=lo,
                                                                      scalar2=None, op0=ALU.is_lt),
                 reads=[('acc', g), ('acch', g), 'lo'], writes=[('notm', g)])
            yield 0.4

    def att_gen(j):
        p = j % 2
        E = EXT[j]
        nkb = E // 128
        pend = None
        for kb in range(nkb):
            s3 = kvn[0] % NKV
            kvn[0] += 1
            S.dma('sp', kTb[s3], kT_d[kb], reads=[('kT_d', kb)], writes=[('kTb', s3)])
            S.dma('sp', vb[s3], v_d[kb].rearrange("p (h c) -> p h c", h=8), reads=[('v_d', kb)], writes=[('vb', s3)])
            g = kb // 4
            pm = kb % 2
            for half in range(2):
                S.op('pe', lambda e, half=half: e.matmul(
                    bank(half), lhsT=NOTM[:, kb * 128:(kb + 1) * 128], rhs=negI.unsqueeze(1).to_broadcast([128, 4, 128]),
                    start=True, stop=False), reads=[('notm', g), 'negI'], writes=[pk(half)])
                for hh in range(4):
                    h = half * 4 + hh
                    S.op('pe', lambda e, h=h, hh=hh, half=half: e.matmul(
                        bank(half)[:, hh * 128:(hh + 1) * 128], lhsT=kTb[s3][:, h * 128:(h + 1) * 128],
                        rhs=qT[p][:, h * 128:(h + 1) * 128], start=False, stop=(hh == 3)),
                         reads=[('kTb', s3), ('qT', p)], writes=[pk(half)])
                if pend is not None:
                    pend()
                S.op('act', lambda e, half=half: e.activation(out=PT[pm][:, half * 512:(half + 1) * 512], in_=bank(half),
                                                              func=AF.Exp, scale=128.0 ** -0.5),
                     reads=[pk(half)], writes=[('PT', pm, half)])

                def pv(kb=kb, pm=pm, s3=s3, half=half):
                    for h in range(half * 4, half * 4 + 4):
                        b, o = pv_slice(h)
                        S.op('pe', lambda e, h=h, o=o: e.matmul(o, lhsT=PT[pm][:, h * 128:(h + 1) * 128], rhs=vb[s3][:, h, :],
                                                                start=(kb == 0 and h % 3 == 0), stop=(kb == nkb - 1)),
                             reads=[('vb', s3), ('PT', pm, half)], writes=[pk(b)])
                pend = pv
            yield 2.0
        pend()
        for bi, b in enumerate(PVB):
            nh = 3 if bi < 2 else 2
            v3 = bank(b)[:, 0:nh * 129].rearrange("p (h c) -> p h c", h=nh)
            S.op('dve', lambda e, v3=v3, bi=bi, nh=nh: e.tensor_scalar(out=rden8[:, bi * 3:bi * 3 + nh], in0=v3[:, :, 128],
                                                                      scalar1=1e-30, scalar2=None, op0=ALU.add),
                 reads=[pk(b)], writes=[('rden8', bi)])
            S.op('dve', lambda e, bi=bi, nh=nh: e.reciprocal(out=rden8[:, bi * 3:bi * 3 + nh], in_=rden8[:, bi * 3:bi * 3 + nh]),
                 reads=[('rden8', bi)], writes=[('rden8', bi)])
            S.op('dve', lambda e, v3=v3, bi=bi, nh=nh: e.tensor_tensor(
                out=yn[:, bi * 3:bi * 3 + nh, :], in0=v3[:, :, 0:128],
                in1=rden8[:, bi * 3:bi * 3 + nh].unsqueeze(2).to_broadcast([128, nh, 128]), op=ALU.mult),
                 reads=[pk(b), ('rden8', bi)], writes=[('yn', bi)])
        pb = bankb(0)
        for h in range(8):
            S.op('pe', lambda e, h=h: e.transpose(pb[:, h * 128:(h + 1) * 128], yn[:, h, :], ident),
                 reads=[('yn', h // 3), 'ident'], writes=[pk(0)])
        S.op('act', lambda e: e.activation(out=yTa[p], in_=pb.rearrange("p (h t) -> p h t", h=8), func=AF.Copy),
             reads=[pk(0)], writes=[('yTa', p)])
        S.dma('pool', yT_d[j][:, 8:16, :], yTa[p], reads=[('yTa', p)], writes=[('yT_d', j, 1)], sem=('yTa', p))
        yield 6.0

    def merge(tasks):
        prog = [0.0] * len(tasks)
        alive = [True] * len(tasks)
        while any(alive):
            best = None
            for i, (g_, tot) in enumerate(tasks):
                if alive[i] and (best is None or prog[i] / tot < prog[best] / tasks[best][1]):
                    best = i
            try:
                prog[best] += next(tasks[best][0])
            except StopIteration:
                alive[best] = False

    merge([(idx_gen(0), 1.0)])
    merge([(ib_gen(0), ib_cost(0))])
    for j in range(1, NSLOT + 1):
        if j < NSLOT:
            merge([(idx_gen(j), 1.0)])
            merge([(att_gen(j - 1), att_cost(j - 1)), (ib_gen(j), ib_cost(j))])
        else:
            merge([(att_gen(j - 1), att_cost(j - 1))])

    S.barrier()
    A.off = base_off

    g_ffn = alloc_g(g_ffn_d)
    Wo = A.alloc([16, 2048], BF16)
    wst = [A.alloc([2048], F32) for _ in range(2)]
    yT = [A.alloc([16, 128], BF16) for _ in range(2)]
    xm = [A.alloc([D], F32) for _ in range(2)]
    x1 = [A.alloc([D], F32) for _ in range(2)]
    hbm = [A.alloc([D], BF16) for _ in range(2)]
    hnTs = [A.alloc([16, 128], BF16) for _ in range(2)]
    load_w(Wo, w_out, 0, 2048, 'Wo', wst)
    def b_mm(j):
        p = j % 2
        S.dma('sp', yT[p], yT_d[j], reads=[('yT_d', j, 0), ('yT_d', j, 1)], writes=[('yT', p)])
        S.dma('sp', xm[p][:, 0:1024], x_own[j, 0:128, 0:1024], writes=[('xm', p, 0)])
        S.dma('sp', xm[p][:, 1024:2048], x_own[j, 0:128, 1024:2048], writes=[('xm', p, 1)])
        for n in range(4):
            b = n
            for kc in range(16):
                S.op('pe', lambda e, kc=kc, n=n, b=b: e.matmul(bank(b), lhsT=yT[p][:, kc, :], rhs=Wo[:, kc, n * 512:(n + 1) * 512],
                                                              start=(kc == 0), stop=(kc == 15)),
                     reads=[('yT', p), ('Wo', kc)], writes=[pk(b)])

    def b_add(j):
        p = j % 2
        for n in range(4):
            b = n
            S.op('dve', lambda e, n=n, b=b: e.tensor_tensor(out=x1[p][:, n * 512:(n + 1) * 512], in0=bank(b),
                                                            in1=xm[p][:, n * 512:(n + 1) * 512], op=ALU.add),
                 reads=[pk(b), ('xm', p, n // 2)], writes=[('x1', p, n)])
        x1k = [('x1', p, n) for n in range(4)]
        S.dma('pool', x1_d[j], x1[p], reads=x1k, writes=[('x1_d', j)], sem=('x1', p))

    def b_norm1(j):
        p = j % 2
        x1k = [('x1', p, n) for n in range(4)]
        rms_h(x1[p], x1k, 128, g_ffn, hbm[p], ('hbm', p), 16 * p)

    def b_norm2(j):
        p = j % 2
        transp16(hbm[p], ('hbm', p), 128, hnTs[p], ('hnTs', p), 0, [4, 5])
        S.dma('pool', hnT_d[j], hnTs[p], reads=[(('hnTs', p), g4) for g4 in range(4)], writes=[('hnT_d', j)], sem=('hnTs', p))

    b_mm(0)
    b_add(0)
    for j in range(NSLOT):
        b_norm1(j)
        if j + 1 < NSLOT:
            b_mm(j + 1)
        b_norm2(j)
        if j + 1 < NSLOT:
            b_add(j + 1)

    S.barrier()
    A.off = base_off

    BT = 6
    TBM = BT * 128
    gT = A.alloc([44, TBM], BF16)
    f_off = A.off
    blocks = [list(range(a, min(NSLOT, a + BT))) for a in range(0, NSLOT, BT)]
    for blk in blocks:
        nt = len(blk)
        TB = nt * 128
        A.off = f_off
        hnT = A.alloc([16, TBM], BF16)
        WG = [A.alloc([16, 256], BF16) for _ in range(2)]
        WV = [A.alloc([16, 256], BF16) for _ in range(2)]
        stg = [A.alloc([4, 256], F32) for _ in range(4)]
        cg = [A.alloc([512], F32) for _ in range(2)]
        cv = [A.alloc([512], F32) for _ in range(2)]
        sg = [A.alloc([512], F32) for _ in range(2)]
        for ti, j in enumerate(blk):
            S.dma('sp', hnT[:, :, ti * 128:(ti + 1) * 128], hnT_d[j], reads=[('hnT_d', j)], writes=[('hnT', ti)])
        hnk = [('hnT', ti) for ti in range(nt)]
        sn = 0
        cn = 0
        bn = 0
        def pre_up(fg_):
            nonlocal_sn = snh
            wp_ = fg_ % 2
            for (Wt, wkey, cbase) in ((WG, 'WG', 0), (WV, 'WV', DFF)):
                for kq in range(4):
                    s = nonlocal_sn[0] % 4
                    nonlocal_sn[0] += 1
                    c0 = cbase + fg_ * 256
                    S.dma('sp', stg[s], w_up[kq * 512:(kq + 1) * 512, c0:c0 + 256].rearrange("(a p) c -> p a c", p=128),
                          writes=[('stg', s)])
                    cast_op(Wt[wp_][:, kq * 4:(kq + 1) * 4, :], stg[s], [('stg', s)], [(wkey, wp_, kq)])

        snh = [0]
        pre_up(0)
        for fg in range(22):
            wp = fg % 2
            if fg + 1 < 22:
                pre_up(fg + 1)
            for fcl in range(2):
                fc = fg * 2 + fcl
                for t0 in range(0, TB, 512):
                    tw = min(512, TB - t0)
                    bG = bn % 8
                    bV = (bn + 1) % 8
                    bn += 2
                    c_ = cn % 2
                    cn += 1
                    for (Wt, wkey, b) in ((WG, 'WG', bG), (WV, 'WV', bV)):
                        for kc in range(16):
                            S.op('pe', lambda e, Wt=Wt, b=b, kc=kc, t0=t0, tw=tw, fcl=fcl: e.matmul(
                                bank(b)[:, 0:tw], lhsT=Wt[wp][:, kc, fcl * 128:(fcl + 1) * 128], rhs=hnT[:, kc, t0:t0 + tw],
                                start=(kc == 0), stop=(kc == 15)),
                                 reads=hnk + [(wkey, wp, kc // 4)], writes=[pk(b)])
                    for (dst, dkey, b, ci) in ((cg[c_], ('cg', c_), bG, fc), (cv[c_], ('cv', c_), bV, 44 + fc)):
                        S.op('act', lambda e, dst=dst, b=b, ci=ci, tw=tw: e.activation(
                            out=dst[:, 0:tw], in_=bank(b)[:, 0:tw], func=AF.Identity, scale=cw[:, 2, ci:ci + 1], bias=cb[:, ci:ci + 1]),
                             reads=[pk(b), 'cw', 'cb'], writes=[dkey])
                        for sh in (1, 2):
                            S.op('dve', lambda e, dst=dst, b=b, ci=ci, tw=tw, sh=sh: e.scalar_tensor_tensor(
                                out=dst[:, sh:tw], in0=bank(b)[:, 0:tw - sh], scalar=cw[:, 2 - sh, ci:ci + 1], in1=dst[:, sh:tw],
                                op0=ALU.mult, op1=ALU.add),
                                 reads=[pk(b), 'cw', dkey], writes=[dkey])
                    S.op('act', lambda e, c_=c_, tw=tw: e.activation(out=sg[c_][:, 0:tw], in_=cg[c_][:, 0:tw], func=AF.Silu),
                         reads=[('cg', c_)], writes=[('sg', c_)])
                    S.op('dve', lambda e, c_=c_, tw=tw, fc=fc, t0=t0: e.tensor_tensor(
                        out=gT[:, fc, t0:t0 + tw], in0=sg[c_][:, 0:tw], in1=cv[c_][:, 0:tw], op=ALU.mult),
                         reads=[('sg', c_), ('cv', c_)], writes=[('gT', fc)])
        S.barrier()
        A.off = f_off
        Wd = [A.alloc([44, 256], BF16) for _ in range(2)]
        stg = [A.alloc([4, 256], F32) for _ in range(4)]
        xr = [A.alloc([256], F32) for _ in range(2)]
        osb = [A.alloc([256], F32) for _ in range(2)]
        gk_all = [('gT', fc) for fc in range(44)]
        on = 0
        sn = 0
        def pre_dn(nb_):
            wp_ = nb_ % 2
            for f4 in range(11):
                s = snd[0] % 4
                snd[0] += 1
                S.dma('sp', stg[s], w_down[f4 * 512:(f4 + 1) * 512, nb_ * 256:(nb_ + 1) * 256].rearrange("(a p) c -> p a c", p=128),
                      writes=[('stg', s)])
                cast_op(Wd[wp_][:, f4 * 4:(f4 + 1) * 4, :], stg[s], [('stg', s)], [('Wd', wp_, f4)])

        snd = [0]
        pre_dn(0)
        for nb in range(8):
            wp = nb % 2
            if nb + 1 < 8:
                pre_dn(nb + 1)
            for ti, j in enumerate(blk):
                o_ = on % 2
                b = on % 8
                on += 1
                S.dma('sp', xr[o_], x1_d[j][:, nb * 256:(nb + 1) * 256], reads=[('x1_d', j)], writes=[('xr', o_)])
                for fc in range(44):
                    S.op('pe', lambda e, b=b, fc=fc, ti=ti, wp=wp: e.matmul(
                        bank(b)[:, 0:256], lhsT=gT[:, fc, ti * 128:(ti + 1) * 128], rhs=Wd[wp][:, fc, :],
                        start=(fc == 0), stop=(fc == 43)),
                         reads=[('gT', fc), ('Wd', wp, fc // 4)], writes=[pk(b)])
                S.op('dve', lambda e, b=b, o_=o_: e.tensor_tensor(out=osb[o_], in0=bank(b)[:, 0:256], in1=xr[o_], op=ALU.add),
                     reads=[pk(b), ('xr', o_)], writes=[('osb', o_)])
                S.dma('pool', out_own[j][:, nb * 256:(nb + 1) * 256], osb[o_], reads=[('osb', o_)],
                      writes=[('out', j, nb)], sem=('osb', o_))
        S.barrier()

    S.emit()
    return nc


def host_inputs(S_len, x, attn_norm_g, w_in, pool_w, pool_scale, q_norm_g, k_norm_g, w_out,
                ffn_norm_g, w_up, conv_w, conv_b, w_down):
    NSLOT, EXT, MLO, MW = slot_geometry(S_len)
    f32 = np.float32
    x2 = np.ascontiguousarray(np.asarray(x, dtype=f32).reshape(S_len, D))
    xpad = np.zeros((17 + 126 * 8 * NSLOT + 128, D), f32)
    xpad[17:17 + S_len] = x2
    siota = np.tile(np.arange(MW, dtype=f32)[None, :], (128, 1))
    pw2 = np.tile((0.5 ** np.arange(1, NIT + 1)).astype(f32)[None, :], (128, 1))
    common = {
        "x_all": x2,
        "siota": siota,
        "pw2": pw2,
        "g_attn": np.asarray(attn_norm_g, f32),
        "g_ffn": np.asarray(ffn_norm_g, f32),
        "w_in": np.asarray(w_in, f32),
        "pool_w": np.asarray(pool_w, f32),
        "pscale": np.ascontiguousarray(np.asarray(pool_scale, f32).reshape(8, 128).T),
        "gq": np.asarray(q_norm_g, f32).reshape(128, 1),
        "gk": np.asarray(k_norm_g, f32).reshape(128, 1),
        "w_out": np.asarray(w_out, f32),
        "w_up": np.asarray(w_up, f32),
        "cw": np.ascontiguousarray(np.asarray(conv_w, f32).reshape(3, 88, 128).transpose(2, 0, 1).reshape(128, 3 * 88)),
        "cb": np.ascontiguousarray(np.asarray(conv_b, f32).reshape(88, 128).T),
        "w_down": np.asarray(w_down, f32),
    }
    wins = (2, 4, 8, 16)
    in_maps = []
    for c in range(8):
        x_own = np.zeros((NSLOT, 143, D), f32)
        tpos = np.zeros((128, NSLOT), f32)
        for j in range(NSLOT):
            T = 8 * j + c
            r0 = 126 * T - 2
            x_own[j, 0:128] = xpad[17 + r0:17 + r0 + 128]
            x_own[j, 128:143] = xpad[17 + r0 - 15:17 + r0]
            tpos[:, j] = r0 + np.arange(128)
        bo = np.zeros((2, 4, 128, 128), f32)
        bh = np.zeros((2, 4, 15, 128), f32)
        for var in range(2):
            r0 = 126 * c - 2 if var == 0 else 10 ** 6
            for g, w in enumerate(wins):
                for t in range(128):
                    pos = r0 + t
                    cntv = float(min(pos + 1, w)) if pos >= 0 else 1.0
                    for i in range(w):
                        tp = t - i
                        if tp >= 0:
                            bo[var, g, tp, t] += 1.0 / cntv
                        else:
                            bh[var, g, 15 + tp, t] += 1.0 / cntv
                    bo[var, g, t, t] -= 1.0
        m = dict(common)
        m["x_own"] = x_own
        m["tpos"] = tpos
        m["band_o"] = np.ascontiguousarray(bo.transpose(2, 0, 1, 3).reshape(128, 2 * 4 * 128))
        m["band_h"] = np.ascontiguousarray(bh.transpose(2, 0, 1, 3).reshape(15, 2 * 4 * 128))
        in_maps.append(m)
    return in_maps


def assemble(S_len, results):
    NSLOT = slot_geometry(S_len)[0]
    out = np.zeros((S_len, D), np.float32)
    for c in range(8):
        o = results[c]["out_own"]
        for j in range(NSLOT):
            T = 8 * j + c
            a = 126 * T
            if a >= S_len:
                continue
            n = min(126, S_len - a)
            out[a:a + n] = o[j, 2:2 + n]
    return out


_CACHE = {}


def run(S_len, **inputs):
    if S_len not in _CACHE:
        _CACHE[S_len] = build_program(S_len)
    nc = _CACHE[S_len]
    in_maps = host_inputs(S_len, **inputs)
    res = run_bass_kernel_spmd(nc, in_maps, core_ids=list(range(8)))
    return assemble(S_len, res.results)


def kernel(x, attn_norm_g, w_in, pool_w, pool_scale, q_norm_g, k_norm_g, w_out,
           ffn_norm_g, w_up, conv_w, conv_b, w_down):
    S_len = x.shape[1]
    out = run(S_len, x=x, attn_norm_g=attn_norm_g, w_in=w_in, pool_w=pool_w, pool_scale=pool_scale,
              q_norm_g=q_norm_g, k_norm_g=k_norm_g, w_out=w_out, ffn_norm_g=ffn_norm_g, w_up=w_up,
              conv_w=conv_w, conv_b=conv_b, w_down=w_down)
    return out.reshape(1, S_len, D)
```
